# Optimizing a Trainium2 kernel written in Bass

```python
import jax, jax.numpy as jnp
from jax import lax
import numpy as np

D_MODEL = 1024
BATCH = 16
SEQ = 2048
DEPTH = 4

N_META = 16
POOL_WINDOWS = (2, 4, 8, 16)
N_POOL_GROUPS = len(POOL_WINDOWS)
POOL_GROUP_DIM = D_MODEL // N_POOL_GROUPS
CONV_WIDTH = 3
D_FF = 2816
N_MIXERS = 2
N_POOL_LAYERS = (DEPTH + 1) // 2
N_CONV_LAYERS = DEPTH // 2
RMS_EPS = 1e-6

kernel_name = "hybrid_pool_shortconv_convffn_trunk"


def rms_norm(x, g):
    xf = x.astype(jnp.float32)
    y = xf * lax.rsqrt(jnp.mean(xf * xf, axis=-1, keepdims=True) + RMS_EPS)
    return (y * g.astype(jnp.float32)).astype(x.dtype)


def causal_dwconv3(x, w):
    L = x.shape[1]
    xp = jnp.pad(x, ((0, 0), (CONV_WIDTH - 1, 0), (0, 0)))
    return w[0] * xp[:, 0:L] + w[1] * xp[:, 1:L + 1] + w[2] * xp[:, 2:L + 2]


def pool_mixer(h, w_group, scale):
    Bsz, L, _ = h.shape
    hf = h.astype(jnp.float32).reshape(Bsz, L, N_POOL_GROUPS, POOL_GROUP_DIM)
    csum = jnp.cumsum(hf, axis=1)
    pos = jnp.arange(L, dtype=jnp.float32)
    outs = []
    for g, w in enumerate(POOL_WINDOWS):
        cg = csum[:, :, g]
        prev = jnp.pad(cg, ((0, 0), (w, 0), (0, 0)))[:, :L]
        count = jnp.minimum(pos + 1.0, float(w))[None, :, None]
        outs.append((cg - prev) / count - hf[:, :, g])
    pooled = jnp.stack(outs, axis=2).astype(h.dtype)
    y = jnp.einsum('blgc,gcd->blgd', pooled, w_group).reshape(Bsz, L, D_MODEL)
    return y * scale


def short_conv_mixer(h, w_in, conv_w, w_out):
    bcv = jnp.einsum('bld,de->ble', h, w_in)
    b_gate, c_gate, v = jnp.split(bcv, 3, axis=-1)
    y = b_gate * causal_dwconv3(c_gate * v, conv_w)
    return jnp.einsum('bld,de->ble', y, w_out)


def conv_ffn(h, w_up, conv_w, w_down):
    up = jnp.einsum('bld,df->blf', h, w_up)
    gate, val = jnp.split(up, 2, axis=-1)
    gate = causal_dwconv3(gate, conv_w)
    return jnp.einsum('blf,fd->bld', jax.nn.silu(gate) * val, w_down)


def setup_inputs(seed: int = 0) -> dict:
    key = jax.random.key(seed)
    ks = jax.random.split(key, 12)
    f32 = jnp.float32
    D = D_MODEL
    x = jax.random.normal(ks[0], (BATCH, SEQ, D), f32)
    meta_tokens = jax.random.normal(ks[1], (N_META, D), f32)
    pool_w = jax.random.normal(ks[2], (N_POOL_LAYERS, N_POOL_GROUPS, POOL_GROUP_DIM, POOL_GROUP_DIM), f32) * POOL_GROUP_DIM ** -0.5
    pool_scale = 1.0 + 0.1 * jax.random.normal(ks[3], (N_POOL_LAYERS, D), f32)
    sc_w_in = jax.random.normal(ks[4], (N_CONV_LAYERS, D, 3 * D), f32) * D ** -0.5
    sc_conv = jax.random.normal(ks[5], (N_CONV_LAYERS, CONV_WIDTH, D), f32) * CONV_WIDTH ** -0.5
    sc_w_out = jax.random.normal(ks[6], (N_CONV_LAYERS, D, D), f32) * D ** -0.5
    ffn_w_up = jax.random.normal(ks[7], (DEPTH, D, 2 * D_FF), f32) * D ** -0.5
    ffn_conv = jax.random.normal(ks[8], (DEPTH, CONV_WIDTH, D_FF), f32) * CONV_WIDTH ** -0.5
    ffn_w_down = jax.random.normal(ks[9], (DEPTH, D_FF, D), f32) * D_FF ** -0.5
    norm_g = 1.0 + 0.05 * jax.random.normal(ks[10], (DEPTH, 4, D), f32)
    return {"x": x, "meta_tokens": meta_tokens, "pool_w": pool_w, "pool_scale": pool_scale,
            "sc_w_in": sc_w_in, "sc_conv": sc_conv, "sc_w_out": sc_w_out,
            "ffn_w_up": ffn_w_up, "ffn_conv": ffn_conv, "ffn_w_down": ffn_w_down,
            "norm_g": norm_g}


def reference(x, meta_tokens, pool_w, pool_scale, sc_w_in, sc_conv, sc_w_out,
              ffn_w_up, ffn_conv, ffn_w_down, norm_g):
    Bsz = x.shape[0]
    meta = jnp.broadcast_to(meta_tokens.astype(x.dtype)[None], (Bsz, N_META, D_MODEL))
    h = jnp.concatenate([meta, x], axis=1)
    for i in range(DEPTH):
        j = i // N_MIXERS
        u = rms_norm(h, norm_g[i, 0])
        if i % N_MIXERS == 0:
            m = pool_mixer(u, pool_w[j], pool_scale[j])
        else:
            m = short_conv_mixer(u, sc_w_in[j], sc_conv[j], sc_w_out[j])
        h = h + rms_norm(m, norm_g[i, 1])
        u = rms_norm(h, norm_g[i, 2])
        f = conv_ffn(u, ffn_w_up[i], ffn_conv[i], ffn_w_down[i])
        h = h + rms_norm(f, norm_g[i, 3])
    return h[:, N_META:]
```

```python
import contextlib
import numpy as np
import concourse.bass as bass
import concourse.mybir as mybir
from concourse.bass_utils import run_bass_kernel_spmd

F32 = mybir.dt.float32
BF16 = mybir.dt.bfloat16
ALU = mybir.AluOpType
AF = mybir.ActivationFunctionType

ENGS = ("pe", "act", "dve", "pool", "sp")

D = 1024
KC = 8
DFF = 2816
JC = 22
NMETA = 16
SEQ = 2048
TSEQ = SEQ + NMETA
GRP = TSEQ // 2
NT = 3
N = GRP // NT
HP = 15
HC = 2
DEPTH = 4
EPS = 1e-6
SCHEDULE = True
INTERLEAVE = False
MMC = 0.155
POOL_W = (2, 4, 8, 16)

PC_G = 0
PC_PS = 128
PC_SC = 144
PC_FC = 192
NPAR = 192 + DEPTH * 3 * JC


def I(name, *args, **kw):
    return lambda e: getattr(e, name)(*args, **kw)


class Op:
    __slots__ = ("idx", "eng", "fns", "deps", "dkey", "signal", "semval", "sem", "cost", "nbytes", "prio")

    def __init__(self, idx, eng, fns, deps, dkey, cost, nbytes):
        self.idx = idx
        self.eng = eng
        self.fns = fns
        self.deps = deps
        self.dkey = dkey
        self.signal = False
        self.semval = 0
        self.sem = None
        self.cost = cost
        self.nbytes = nbytes
        self.prio = 0


SCHED_WINDOW = {"pe": 8, "act": 8, "dve": 8, "pool": 10, "sp": 6}


class Prog:
    def __init__(self):
        self.ops = []
        self.last_w = {}
        self.readers = {}
        self.last_dma = {}
        self.order = None
        self.prio = 0

    def op(self, eng, fn, reads=(), writes=(), dkey=None, cost=0.5, nbytes=0):
        deps = set()
        for r in reads:
            w = self.last_w.get(r)
            if w is not None:
                deps.add(w)
        for r in writes:
            w = self.last_w.get(r)
            if w is not None:
                deps.add(w)
            for q in self.readers.get(r, ()):
                deps.add(q)
        if dkey is not None:
            p = self.last_dma.get(dkey)
            if p is not None:
                deps.add(p)
        idx = len(self.ops)
        fns = fn if isinstance(fn, (list, tuple)) else [fn]
        o = Op(idx, eng, fns, deps, dkey, cost, nbytes)
        o.prio = self.prio
        self.ops.append(o)
        for r in writes:
            self.last_w[r] = idx
            self.readers[r] = []
        for r in reads:
            if r not in writes:
                self.readers.setdefault(r, []).append(idx)
        if dkey is not None:
            self.last_dma[dkey] = idx
        return idx

    def alias(self, old_keys, new_keys):
        s = set()
        for k in old_keys:
            w = self.last_w.pop(k, None)
            if w is not None:
                s.add(w)
            s.update(self.readers.pop(k, ()))
        for k in new_keys:
            self.readers.setdefault(k, []).extend(s)

    def schedule(self):
        ops = self.ops
        n = len(ops)
        lists = {e: [o.idx for o in ops if o.eng == e] for e in ENGS}
        head = {e: 0 for e in ENGS}
        tfree = {e: 0.0 for e in ENGS}
        done = [False] * n
        finish = [0.0] * n
        order = {e: [] for e in ENGS}
        dma_t = 0.0
        remaining = n
        while remaining:
            best = None
            for e in ENGS:
                lst = lists[e]
                hd = head[e]
                while hd < len(lst) and done[lst[hd]]:
                    hd += 1
                head[e] = hd
                cnt = 0
                i = hd
                tf = tfree[e]
                W = SCHED_WINDOW[e]
                cand = None
                while i < len(lst) and cnt < W:
                    k = lst[i]
                    i += 1
                    if done[k]:
                        continue
                    cnt += 1
                    r = 0.0
                    ok = True
                    for d in ops[k].deps:
                        if not done[d]:
                            ok = False
                            break
                        f = finish[d]
                        if f > r:
                            r = f
                    if not ok:
                        continue
                    pr = ops[k].prio
                    if r <= tf:
                        key = (0, pr, k)
                    else:
                        key = (1, r, pr, k)
                    if cand is None or key < cand[0]:
                        cand = (key, r if r > tf else tf, k)
                    if r <= tf and pr == 0:
                        break
                if cand is not None:
                    st, k = cand[1], cand[2]
                    if best is None or st < best[0] or (st == best[0] and k < best[1]):
                        best = (st, k, e)
            st, k, e = best
            o = ops[k]
            if o.dkey is not None:
                tfree[e] = st + o.cost
                dma_t = max(dma_t, st + o.cost) + o.nbytes / 330e3
                finish[k] = dma_t + 2.0
            else:
                tfree[e] = st + o.cost
                finish[k] = st + o.cost + (0.25 if e == "pe" else 0.15)
            done[k] = True
            order[e].append(k)
            remaining -= 1
        self.order = order
        return max(finish) if n else 0.0

    def emit(self, nc, final_wait_ops=()):
        ops = self.ops
        order = self.order or {e: [o.idx for o in ops if o.eng == e] for e in ENGS}
        sem_deps = {}
        for o in ops:
            if o.eng == "pe" and o.dkey is None:
                sd = [d for d in o.deps if not (ops[d].eng == "pe" and ops[d].dkey is None)]
            else:
                sd = list(o.deps)
            sem_deps[o.idx] = sd
            for d in sd:
                ops[d].signal = True
        for d in final_wait_ops:
            ops[d].signal = True
        for o in ops:
            if o.dkey is not None:
                o.signal = True
        dkeys = sorted({o.dkey for o in ops if o.dkey is not None})
        with contextlib.ExitStack() as es:
            esem = {e: es.enter_context(nc.semaphore("s_" + e)) for e in ENGS}
            dsem = {k: es.enter_context(nc.semaphore("d_" + k)) for k in dkeys}
            dcount = {k: 0 for k in dkeys}
            for o in ops:
                if o.dkey is not None:
                    dcount[o.dkey] += 16
                    o.sem = dsem[o.dkey]
                    o.semval = dcount[o.dkey]
            for e in ENGS:
                c = 0
                for k in order[e]:
                    o = ops[k]
                    if o.signal and o.dkey is None:
                        c += 1
                        o.sem = esem[e]
                        o.semval = c
            block = es.enter_context(nc.Block())
            fin = [ops[d] for d in final_wait_ops]

            def make(ename, with_final):
                def body(eng):
                    waited = {}
                    for k in order[ename]:
                        o = ops[k]
                        need = {}
                        for d in sem_deps[k]:
                            p = ops[d]
                            kk = id(p.sem)
                            if p.semval > need.get(kk, (0, None))[0]:
                                need[kk] = (p.semval, p.sem)
                        for kk, (v, sm) in need.items():
                            if waited.get(kk, 0) >= v:
                                continue
                            eng.wait_ge(sm, v)
                            waited[kk] = v
                        ins = None
                        for fn in o.fns:
                            ins = fn(eng)
                        if o.signal:
                            ins.then_inc(o.sem, 16 if o.dkey is not None else 1)
                    if with_final:
                        for p in fin:
                            eng.wait_ge(p.sem, p.semval)
                return body

            block.tensor(make("pe", False))
            block.scalar(make("act", False))
            block.vector(make("dve", False))
            block.gpsimd(make("pool", False))
            block.sync(make("sp", True))


def build(layers=(0, 1, 2, 3), ngroups=4):
    nc = bass.Bass("TRN2", target_bir_lowering=False)
    nseq = (ngroups + 1) // 2
    hin = nc.dram_tensor("hin", [nseq, TSEQ, D], F32, kind="ExternalInput").ap()
    par_d = nc.dram_tensor("params", [128, NPAR], F32, kind="ExternalInput").ap()
    cnt_d = nc.dram_tensor("invcnt", [128, 4 * 16], F32, kind="ExternalInput").ap()
    idn_d = nc.dram_tensor("ident", [128, 128], F32, kind="ExternalInput").ap()
    pool_w = nc.dram_tensor("pool_w", [2, 4, 256, 256], F32, kind="ExternalInput").ap()
    sc_w_in = nc.dram_tensor("sc_w_in", [2, D, 3 * D], F32, kind="ExternalInput").ap()
    sc_w_out = nc.dram_tensor("sc_w_out", [2, D, D], F32, kind="ExternalInput").ap()
    w_up = nc.dram_tensor("ffn_w_up", [DEPTH, D, 2 * DFF], F32, kind="ExternalInput").ap()
    w_down = nc.dram_tensor("ffn_w_down", [DEPTH, DFF, D], F32, kind="ExternalInput").ap()
    out = nc.dram_tensor("out", [nseq, TSEQ, D], F32, kind="ExternalOutput").ap()

    P = Prog()
    es = contextlib.ExitStack()
    with es:
        def sb(name, shape, dt):
            return es.enter_context(nc.sbuf_tensor(name, shape, dt))

        h = sb("h", [128, KC, GRP], F32)
        actbuf = sb("actbuf", [128, JC * GRP // 2], F32)
        ugrp = sb("ugrp", [128, KC, GRP + HC], BF16)
        wa = sb("wa", [128, JC, D], BF16)
        ring = [sb(f"ring{i}", [128, 2 * KC * 384], BF16) for i in range(2)]
        sq = sb("sq", [128, KC, N + HC], BF16)
        NVEC = 16
        vbuf = sb("vbuf", [128, NVEC, N + HC], F32)
        vecs = [vbuf[:, i, :] for i in range(NVEC)]
        par = sb("par", [128, NPAR], F32)
        cnt = sb("cnt", [128, 4, 16], F32)
        idn = sb("idn", [128, 128], F32)
        ones = sb("ones", [128, 128], BF16)
        stg = [sb(f"stg{i}", [128, D], F32) for i in range(2)]
        halo_f = {l: sb(f"halof{l}", [128, KC, HC], BF16) for l in range(DEPTH)}
        halo_m = {}
        for l in range(DEPTH):
            if l % 2 == 0:
                halo_m[l] = sb(f"halom{l}", [128, KC, HP], F32)
            else:
                halo_m[l] = sb(f"halom{l}", [128, KC, HC], BF16)
        ps = [es.enter_context(nc.psum_tensor(f"ps{i}", [128, 512], F32)) for i in range(8)]

        ab16 = actbuf[:, :].bitcast(BF16)
        RW16 = JC * N
        RW32 = RW16 // 2
        act_t = [ab16[:, t * RW16:(t + 1) * RW16].rearrange("p (j n) -> p j n", j=JC) for t in range(NT)]
        y_t = [ab16[:, t * RW16:t * RW16 + KC * N].rearrange("p (j n) -> p j n", j=KC) for t in range(NT)]
        PW = 2 * (N + HP)
        assert 4 * PW <= RW32
        pscr = [[actbuf[:, t * RW32 + i * PW:t * RW32 + (i + 1) * PW].rearrange("p (c n) -> p c n", c=2)
                 for i in range(4)] for t in range(NT)]
        reg_keys = [{"act": [f"act{t}_{j}" for j in range(JC)],
                     "y": [f"y{t}_{m}" for m in range(KC)],
                     "pool": [f"pscr{t}_{i}" for i in range(4)]} for t in range(NT)]
        reg_mode = ["act"] * NT

        def region_to(t, mode):
            if reg_mode[t] != mode:
                P.alias(reg_keys[t][reg_mode[t]], reg_keys[t][mode])
                reg_mode[t] = mode

        cnts = {"psA": 0, "psB": 0, "psC": 0, "psC6": 0, "psS": 0, "A3": 0, "B3": 0, "S1c": 0, "S1v": 0, "S1b": 0, "vec": 0, "ring": 0, "stg": 0, "alt": 0}

        def g_ap(l, i, c):
            col = PC_G + (l * 4 + i) * 8 + c
            return par[:, col:col + 1]

        def ps_ap(j, c):
            col = PC_PS + j * 8 + c
            return par[:, col:col + 1]

        def sc_ap(j, k, c):
            col = PC_SC + (j * 3 + k) * 8 + c
            return par[:, col:col + 1]

        def fc_ap(l, k, jc):
            col = PC_FC + (l * 3 + k) * JC + jc
            return par[:, col:col + 1]

        P.op("sp", I("dma_start", out=par[:], in_=par_d), writes=["par"], dkey="par")
        P.op("sp", I("dma_start", out=cnt[:].rearrange("p g s -> p (g s)"), in_=cnt_d), writes=["cnt"], dkey="cnt")
        P.op("sp", I("dma_start", out=idn[:], in_=idn_d), writes=["idn"], dkey="idn")
        P.op("dve", I("memset", ones[:], 1.0), writes=["ones"])
        epsb = sb("epsb", [128, 1], F32)
        eps_ap = epsb[:, 0:1]
        P.op("dve", I("memset", epsb[:], EPS), writes=["eps"])

        def bank(role):
            if role in ("S", "T"):
                i = 6 + (cnts["psS"] % 2)
                cnts["psS"] += 1
                return i, f"ps{i}"
            if role == "C6":
                i = (4, 5, 0, 1, 2, 3)[cnts["psC6"] % 6]
                cnts["psC6"] += 1
                return i, f"ps{i}"
            if role in ("S1c", "S1v", "S1b"):
                bl = {"S1c": (0, 1), "S1v": (2,), "S1b": (3, 4, 5)}[role]
                i = bl[cnts[role] % len(bl)]
                cnts[role] += 1
                return i, f"ps{i}"
            if role in ("A3", "B3"):
                i = {"A3": (0, 1, 4, 6), "B3": (2, 3, 5, 7)}[role][cnts[role] % 4]
                cnts[role] += 1
                return i, f"ps{i}"
            base = {"A": 0, "B": 2, "C": 4}[role]
            k = "ps" + role
            i = base + (cnts[k] % 2)
            cnts[k] += 1
            return i, f"ps{i}"

        def vec():
            i = cnts["vec"] % NVEC
            cnts["vec"] += 1
            return vecs[i], f"vec{i}"

        def ring_slot():
            i = cnts["ring"] % 2
            cnts["ring"] += 1
            return ring[i], f"ring{i}"

        def load_wup_slot(l, sg):
            j0 = 3 * sg
            nj = min(3, JC - j0)
            slot, key = ring_slot()
            sv = slot[:, 0:2 * KC * 384].rearrange("p (g k c) -> p g k c", g=2, k=KC)
            for gv in range(2):
                c0 = gv * DFF + j0 * 128
                src = w_up[l, :, c0:c0 + nj * 128].rearrange("(k p) c -> p k c", p=128)
                dst = sv[:, gv, :, 0:nj * 128]
                P.op("pool", I("dma_start", out=dst, in_=src),
                     writes=[f"{key}_b{u}" for u in range(3 * gv, 3 * gv + 3)], dkey=key + f"_{gv}", cost=1.0,
                     nbytes=nj * 128 * D * 4)
            return sv, key, j0, nj

        def load_wa_down(l):
            bounds = [0, 6, 12, 17, 22]
            for pi in range(4):
                a, b = bounds[pi], bounds[pi + 1]
                src = w_down[l, a * 128:b * 128, :].rearrange("(j p) c -> p j c", p=128)
                dst = wa[:, a:b, :]
                P.op("pool", I("dma_start", out=dst, in_=src),
                     writes=[f"wa{pi}"], dkey=f"wa{pi}", cost=1.0, nbytes=(b - a) * 128 * D * 4)

        def load_wa_out(jl):
            for pi, (a, b) in enumerate(((0, 6), (6, 8))):
                src = sc_w_out[jl, a * 128:b * 128, :].rearrange("(j p) c -> p j c", p=128)
                dst = wa[:, a:b, :]
                P.op("pool", I("dma_start", out=dst, in_=src),
                     writes=[f"wa{pi}"], dkey=f"wa{pi}", cost=1.0, nbytes=(b - a) * 128 * D * 4)

        def load_win_slot(jl, sg):
            slot, key = ring_slot()
            sv = slot[:, 0:3 * KC * 256].rearrange("p (g k c) -> p g k c", g=3, k=KC)
            for gi in range(3):
                c0 = gi * D + sg * 256
                src = sc_w_in[jl, :, c0:c0 + 256].rearrange("(k p) c -> p k c", p=128)
                dst = sv[:, gi, :, :]
                P.op("pool", I("dma_start", out=dst, in_=src),
                     writes=[f"{key}_b{u}" for u in range(2 * gi, 2 * gi + 2)], dkey=key + f"_{gi}", cost=1.0,
                     nbytes=256 * D * 4)
            return sv, key

        pwres = sb("pwres", [128, 2, 4, 2, 256], BF16)
        for jl in range(2):
            for g in range(4):
                P.op("pool", I("dma_start", out=pwres[:, jl, g, :, :], in_=pool_w[jl, g].rearrange("(k p) c -> p k c", p=128)),
                     writes=[f"pw{jl}"], dkey=f"pw{jl}", cost=1.0, nbytes=256 * 256 * 4)

        def rstd_from_ss(sbank, skey, ncols):
            t1, k1 = vec()
            P.op("act", I("activation", out=t1[:, 0:ncols], in_=ps[sbank][:, 0:ncols], func=AF.Ln,
                          bias=eps_ap, scale=1.0 / D),
                 reads=[skey, "eps"], writes=[k1], cost=1.9)
            P.op("act", I("activation", out=t1[:, 0:ncols], in_=t1[:, 0:ncols], func=AF.Exp, scale=-0.5),
                 reads=[k1], writes=[k1], cost=1.9)
            return t1, k1

        def hk(t):
            return f"h{t}"

        sq_mode = ["whole"]

        def sq_to(mode):
            if sq_mode[0] == mode:
                return
            if mode == "chunks":
                P.alias(["sq"], [f"sqm{m}" for m in range(KC)])
            else:
                P.alias([f"sqm{m}" for m in range(KC)], ["sq"])
            sq_mode[0] = mode

        def norm_stats(t):
            o = t * N
            sq_to("whole")
            P.op("act", I("activation", out=sq[:, :, 0:N], in_=h[:, :, o:o + N], func=AF.Square),
                 reads=[hk(t)], writes=["sq"], cost=2.2)
            sbank, skey = bank("S")
            P.op("pe", [I("matmul", ps[sbank][:, 0:N], lhsT=ones[:], rhs=sq[:, c, 0:N],
                          start=(c == 0), stop=(c == KC - 1)) for c in range(KC)],
                 reads=["sq", "ones"], writes=[skey], cost=KC * MMC)
            return rstd_from_ss(sbank, skey, N)

        def prenorm_ugrp(l, i, t):
            o = t * N
            r, kr = norm_stats(t)
            for c in range(KC):
                P.op("dve", I("scalar_tensor_tensor", out=ugrp[:, c, HC + o:HC + o + N], in0=h[:, c, o:o + N],
                              scalar=g_ap(l, i, c), in1=r[:, 0:N], op0=ALU.mult, op1=ALU.mult),
                     reads=[hk(t), kr, "par"], writes=[f"ug{t}"], cost=0.5)

        class PostNorm:
            def __init__(self, l, i, t, scale_fn=None):
                self.l, self.i, self.t = l, i, t
                self.scale_fn = scale_fn
                sq_to("chunks")
                self.sbank, self.skey = bank("S")
                self.bufs = {}

            def chunk(self, m, pbank, pkey):
                sc = self.scale_fn(m) if self.scale_fn is not None else 1.0
                rd = [pkey] + (["par"] if self.scale_fn is not None else [])
                fr, kf = vec()
                self.bufs[m] = (fr, kf)
                P.op("act", I("activation", out=fr[:, 0:N], in_=ps[pbank][:, 0:N], func=AF.Copy, scale=sc),
                     reads=rd, writes=[kf], cost=0.55)
                P.op("act", I("activation", out=sq[:, m, 0:N], in_=ps[pbank][:, 0:N], func=AF.Square, scale=sc),
                     reads=rd, writes=[f"sqm{m}"], cost=0.45)
                P.op("pe", I("matmul", ps[self.sbank][:, 0:N], lhsT=ones[:], rhs=sq[:, m, 0:N],
                             start=(m == 0), stop=(m == KC - 1)),
                     reads=[f"sqm{m}", "ones"], writes=[self.skey], cost=MMC)

            def finish(self):
                l, i, t = self.l, self.i, self.t
                o = t * N
                r, kr = rstd_from_ss(self.sbank, self.skey, N)
                for m in range(KC):
                    fr, kf = self.bufs[m]
                    P.op("dve", I("scalar_tensor_tensor", out=fr[:, 0:N], in0=fr[:, 0:N], scalar=g_ap(l, i, m),
                                  in1=r[:, 0:N], op0=ALU.mult, op1=ALU.mult),
                         reads=[kf, kr, "par"], writes=[kf], cost=0.58)
                    P.op("pool", I("tensor_tensor", out=h[:, m, o:o + N], in0=h[:, m, o:o + N], in1=fr[:, 0:N], op=ALU.add),
                         reads=[kf, hk(t)], writes=[hk(t)], cost=0.95)

        NBLK = (GRP + 127) // 128

        def blk(tb):
            r0 = tb * 128
            return r0, min(128, GRP - r0)

        def tiles_of(tb):
            r0, nr = blk(tb)
            return sorted({min(NT - 1, r // N) for r in (r0, r0 + nr - 1)})

        def load_block(gi, tb):
            b, half = gi // 2, gi % 2
            t0 = half * GRP
            r0, nr = blk(tb)
            si = cnts["stg"] % 2
            cnts["stg"] += 1
            s = stg[si]
            P.op("sp", I("dma_start", out=s[0:nr, :], in_=hin[b, t0 + r0:t0 + r0 + nr, :]),
                 writes=[f"stg{si}"], dkey=f"ld{si}", cost=0.1, nbytes=nr * D * 4)
            hkeys = [hk(t) for t in tiles_of(tb)]
            for hf in range(2):
                tb_i, tkey = bank("T")
                pv = ps[tb_i][:, 0:4 * 128].rearrange("p (c t) -> p c t", c=4)
                P.op("pe", [I("transpose", out=pv[:, cc, 0:nr], in_=s[0:nr, (hf * 4 + cc) * 128:(hf * 4 + cc + 1) * 128],
                              identity=idn[0:nr, 0:nr]) for cc in range(4)],
                     reads=[f"stg{si}", "idn"], writes=[tkey], cost=1.2)
                P.op("act", I("activation", out=h[:, hf * 4:hf * 4 + 4, r0:r0 + nr], in_=pv[:, :, 0:nr], func=AF.Copy),
                     reads=[tkey], writes=hkeys, cost=0.6)

        out_dmas = {}

        def store_block(gi, tb):
            b, half = gi // 2, gi % 2
            t0 = half * GRP
            r0, nr = blk(tb)
            si = cnts["stg"] % 2
            cnts["stg"] += 1
            s = stg[si]
            hkeys = [hk(t) for t in tiles_of(tb)]
            for hf in range(2):
                tb_i, tkey = bank("T")
                P.op("pe", [I("transpose", out=ps[tb_i][0:nr, cc * 128:(cc + 1) * 128], in_=h[:, hf * 4 + cc, r0:r0 + nr],
                              identity=idn[:]) for cc in range(4)],
                     reads=hkeys + ["idn"], writes=[tkey], cost=1.2)
                P.op("act", I("activation", out=s[0:nr, hf * 512:(hf + 1) * 512], in_=ps[tb_i][0:nr, 0:512], func=AF.Copy),
                     reads=[tkey], writes=[f"stg{si}"], cost=0.6)
            d = P.op("sp", I("dma_start", out=out[b, t0 + r0:t0 + r0 + nr, :], in_=s[0:nr, :]),
                     reads=[f"stg{si}"], dkey=f"st{si}", cost=0.1, nbytes=nr * D * 4)
            out_dmas[f"st{si}"] = d

        def ugrp_halo_fill(src, srckey, half):
            if half == 0:
                P.op("dve", I("memset", ugrp[:, :, 0:HC], 0.0), writes=["ugh"], cost=0.1)
            else:
                P.op("dve", I("tensor_copy", out=ugrp[:, :, 0:HC], in_=src[:, :, :]),
                     reads=[srckey], writes=["ugh"], cost=0.1)

        def ugrp_halo_save(dst, dstkey):
            P.op("dve", I("tensor_copy", out=dst[:, :, :], in_=ugrp[:, :, GRP:GRP + HC]),
                 reads=[f"ug{NT - 1}"], writes=[dstkey], cost=0.1)

        def ffn_pre_tile(l, t, half):
            if t == 0:
                ugrp_halo_fill(halo_f[l], f"halof{l}", half)
            prenorm_ugrp(l, 2, t)
            if t == NT - 1:
                ugrp_halo_save(halo_f[l], f"halof{l}")

        def sconv_pre_tile(l, t, half):
            if t == 0:
                ugrp_halo_fill(halo_m[l], f"halom{l}", half)
            prenorm_ugrp(l, 0, t)
            if t == NT - 1:
                ugrp_halo_save(halo_m[l], f"halom{l}")

        def ukeys_of(t):
            return [f"ug{t}"] + ([f"ug{t - 1}"] if t > 0 else ["ugh"])

        NSG = (JC + 2) // 3
        pre_wup = {}
        pre_win = {}

        def ffn_prefetch(l):
            pre_wup[l] = [load_wup_slot(l, 0), load_wup_slot(l, 1)]

        def sconv_prefetch(l):
            pre_win[l] = [load_win_slot(l // 2, 0), load_win_slot(l // 2, 1)]

        def ffn_u(l):
            for t in range(NT):
                region_to(t, "act")
            load_wa_down(l)
            if l not in pre_wup:
                ffn_prefetch(l)
            slots = pre_wup.pop(l)
            nsg = NSG
            for sg in range(nsg):
                sv, key, j0, nj = slots[sg]
                for t in range(NT):
                    o = t * N
                    for jj in range(nj):
                        j = j0 + jj
                        ga, gk = bank("A3")
                        va, vk = bank("B3")
                        P.op("pe", [I("matmul", ps[ga][:, 0:N + HC], lhsT=sv[:, 0, k, jj * 128:(jj + 1) * 128],
                                      rhs=ugrp[:, k, o:o + N + HC], start=(k == 0), stop=(k == KC - 1)) for k in range(KC)],
                             reads=ukeys_of(t) + [f"{key}_b{u}" for u in range(0, 3)], writes=[gk], cost=KC * MMC)
                        P.op("pe", [I("matmul", ps[va][:, 0:N], lhsT=sv[:, 1, k, jj * 128:(jj + 1) * 128],
                                      rhs=ugrp[:, k, o + HC:o + HC + N], start=(k == 0), stop=(k == KC - 1)) for k in range(KC)],
                             reads=[f"ug{t}"] + [f"{key}_b{u}" for u in range(3, 6)], writes=[vk], cost=KC * MMC)
                        a, ka = vec()
                        P.op("act", I("activation", out=a[:, 0:N], in_=ps[ga][:, 2:N + 2], func=AF.Copy, scale=fc_ap(l, 2, j)),
                             reads=[gk, "par"], writes=[ka], cost=0.5)
                        P.op("dve", I("scalar_tensor_tensor", out=a[:, 0:N], in0=ps[ga][:, 1:N + 1], scalar=fc_ap(l, 1, j),
                                      in1=a[:, 0:N], op0=ALU.mult, op1=ALU.add),
                             reads=[gk, "par", ka], writes=[ka], cost=0.5)
                        P.op("dve", I("scalar_tensor_tensor", out=a[:, 0:N], in0=ps[ga][:, 0:N], scalar=fc_ap(l, 0, j),
                                      in1=a[:, 0:N], op0=ALU.mult, op1=ALU.add),
                             reads=[gk, "par", ka], writes=[ka], cost=0.5)
                        s_, ks = vec()
                        P.op("act", I("activation", out=s_[:, 0:N], in_=a[:, 0:N], func=AF.Silu),
                             reads=[ka], writes=[ks], cost=0.5)
                        P.op("dve", I("tensor_tensor", out=act_t[t][:, j, :], in0=s_[:, 0:N], in1=ps[va][:, 0:N], op=ALU.mult),
                             reads=[ks, vk], writes=[f"act{t}_{j}"], cost=0.5)
                if sg + 2 < nsg:
                    slots.append(load_wup_slot(l, sg + 2))

        def ffn_d_tile(l, t):
            pn = PostNorm(l, 3, t)
            for m in range(KC):
                fb, fk = bank("C6")
                P.op("pe", [I("matmul", ps[fb][:, 0:N], lhsT=wa[:, j, m * 128:(m + 1) * 128], rhs=act_t[t][:, j, :],
                              start=(j == 0), stop=(j == JC - 1)) for j in range(JC)],
                     reads=[f"act{t}_{j}" for j in range(JC)] + ["wa0", "wa1", "wa2", "wa3"], writes=[fk], cost=JC * MMC)
                pn.chunk(m, fb, fk)
            pn.finish()

        def sconv_s1(l):
            jl = l // 2
            for t in range(NT):
                region_to(t, "y")
            if l not in pre_win:
                sconv_prefetch(l)
            slots = pre_win.pop(l)
            load_wa_out(jl)
            for sg in range(4):
                sv, key = slots[sg]
                for t in range(NT):
                    o = t * N
                    for mm in range(2):
                        m = 2 * sg + mm
                        ca, ck = bank("S1c")
                        va, vk = bank("S1v")
                        ba, bk = bank("S1b")
                        for gi, (bb, ncol, off) in enumerate(((ba, N, HC), (ca, N + HC, 0), (va, N + HC, 0))):
                            P.op("pe", [I("matmul", ps[bb][:, 0:ncol], lhsT=sv[:, gi, k, mm * 128:(mm + 1) * 128],
                                          rhs=ugrp[:, k, o + off:o + off + ncol], start=(k == 0), stop=(k == KC - 1))
                                        for k in range(KC)],
                                 reads=ukeys_of(t) + [f"{key}_b{u}" for u in range(2 * gi, 2 * gi + 2)],
                                 writes=[(bk, ck, vk)[gi]], cost=KC * MMC)
                        vs, kvs = vec()
                        P.op("act", I("activation", out=vs[:, 0:N + HC], in_=ps[va][:, 0:N + HC], func=AF.Copy),
                             reads=[vk], writes=[kvs], cost=0.5)
                        pr, kp = vec()
                        P.op("dve", I("tensor_tensor", out=pr[:, 0:N + HC], in0=vs[:, 0:N + HC], in1=ps[ca][:, 0:N + HC], op=ALU.mult),
                             reads=[kvs, ck], writes=[kp], cost=0.5)
                        a, ka = vec()
                        P.op("act", I("activation", out=a[:, 0:N], in_=pr[:, 2:N + 2], func=AF.Copy, scale=sc_ap(jl, 2, m)),
                             reads=[kp, "par"], writes=[ka], cost=0.5)
                        P.op("dve", I("scalar_tensor_tensor", out=a[:, 0:N], in0=pr[:, 1:N + 1], scalar=sc_ap(jl, 1, m),
                                      in1=a[:, 0:N], op0=ALU.mult, op1=ALU.add),
                             reads=[kp, "par", ka], writes=[ka], cost=0.5)
                        P.op("dve", I("scalar_tensor_tensor", out=a[:, 0:N], in0=pr[:, 0:N], scalar=sc_ap(jl, 0, m),
                                      in1=a[:, 0:N], op0=ALU.mult, op1=ALU.add),
                             reads=[kp, "par", ka], writes=[ka], cost=0.5)
                        P.op("dve", I("tensor_tensor", out=y_t[t][:, m, :], in0=a[:, 0:N], in1=ps[ba][:, 0:N], op=ALU.mult),
                             reads=[ka, bk], writes=[f"y{t}_{m}"], cost=0.5)
                if sg + 2 < 4:
                    slots.append(load_win_slot(jl, sg + 2))

        def sconv_s2_tile(l, t):
            pn = PostNorm(l, 1, t)
            for m in range(KC):
                fb, fk = bank("C6")
                P.op("pe", [I("matmul", ps[fb][:, 0:N], lhsT=wa[:, k, m * 128:(m + 1) * 128], rhs=y_t[t][:, k, :],
                              start=(k == 0), stop=(k == KC - 1)) for k in range(KC)],
                     reads=[f"y{t}_{k}" for k in range(KC)] + ["wa0", "wa1"], writes=[fk], cost=KC * MMC)
                pn.chunk(m, fb, fk)
            pn.finish()

        pool_r = {}

        def pool_stage_a(l, t, half):
            region_to(t, "pool")
            pool_r[t] = norm_stats(t)

        def pool_stage_b(l, t, half):
            o = t * N
            first = (half == 0 and t == 0)
            r, kr = pool_r[t]
            for g in range(4):
                W = POOL_W[g]
                pu = pscr[t][g % 2]
                pukey = f"pscr{t}_{g % 2}"
                hkey = f"halom{l}_{g}"
                if first:
                    P.op("dve", I("memset", pu[:, :, 0:HP], 0.0), writes=[pukey], cost=0.1)
                else:
                    P.op("dve", I("tensor_copy", out=pu[:, :, 0:HP], in_=halo_m[l][:, 2 * g:2 * g + 2, :]),
                         reads=[hkey], writes=[pukey], cost=0.1)
                for cc in range(2):
                    c = 2 * g + cc
                    P.op("dve", I("scalar_tensor_tensor", out=pu[:, cc, HP:HP + N], in0=h[:, c, o:o + N],
                                  scalar=g_ap(l, 0, c), in1=r[:, 0:N], op0=ALU.mult, op1=ALU.mult),
                         reads=[hk(t), kr, "par", pukey], writes=[pukey], cost=0.58)
                P.op("dve", I("tensor_copy", out=halo_m[l][:, 2 * g:2 * g + 2, :], in_=pu[:, :, N:N + HP]),
                     reads=[pukey], writes=[hkey], cost=0.1)
                cur, lo, curkey = pu, 0, pukey
                for s_ in range(g + 1):
                    sh = 1 << s_
                    nlo = HP - (W - 2 * sh)
                    ln = HP + N - nlo
                    dstb = pscr[t][2 + (s_ % 2)]
                    dkey_ = f"pscr{t}_{2 + (s_ % 2)}"
                    eng = "dve"
                    P.op(eng, I("tensor_tensor", out=dstb[:, :, 0:ln], in0=cur[:, :, nlo - lo:nlo - lo + ln],
                                in1=cur[:, :, nlo - lo - sh:nlo - lo - sh + ln], op=ALU.add),
                         reads=[curkey], writes=[dkey_], cost=0.9)
                    cur, lo, curkey = dstb, nlo, dkey_
                sfin = cur[:, :, HP - lo:HP - lo + N]
                pl = ugrp[:, 2 * g:2 * g + 2, HC + o:HC + o + N]
                P.op("dve", I("scalar_tensor_tensor", out=pl, in0=sfin, scalar=1.0 / W,
                              in1=pu[:, :, HP:HP + N], op0=ALU.mult, op1=ALU.subtract),
                     reads=[curkey, pukey], writes=[f"ug{t}"], cost=0.9)
                if first:
                    for cc in range(2):
                        tmpv, kt = vec()
                        P.op("dve", I("tensor_tensor", out=tmpv[:, 0:16], in0=sfin[:, cc, 0:16], in1=cnt[:, g, :], op=ALU.mult),
                             reads=[curkey, "cnt"], writes=[kt], cost=0.1)
                        P.op("dve", I("tensor_tensor", out=ugrp[:, 2 * g + cc, HC + o:HC + o + 16], in0=tmpv[:, 0:16],
                                      in1=pu[:, cc, HP:HP + 16], op=ALU.subtract),
                             reads=[kt, pukey], writes=[f"ug{t}"], cost=0.1)

        def pool_stage_c(l, t, half):
            jl = l // 2
            o = t * N
            pn = PostNorm(l, 1, t, scale_fn=lambda m: ps_ap(jl, m))
            for g in range(4):
                for dh in range(2):
                    fb, fk = bank("C6")
                    P.op("pe", [I("matmul", ps[fb][:, 0:N], lhsT=pwres[:, jl, g, kc, dh * 128:(dh + 1) * 128],
                                  rhs=ugrp[:, 2 * g + kc, HC + o:HC + o + N], start=(kc == 0), stop=(kc == 1)) for kc in range(2)],
                         reads=[f"ug{t}", f"pw{jl}"], writes=[fk], cost=2 * MMC)
                    pn.chunk(2 * g + dh, fb, fk)
            pn.finish()

        def pool_phase(l, half):
            stages = (pool_stage_a, pool_stage_b, pool_stage_c, ffn_pre_tile)
            for step in range(NT + len(stages) - 1):
                for k, st in enumerate(stages):
                    t = step - k
                    if 0 <= t < NT:
                        st(l, t, half)

        def pool_tile(l, t, half):
            pool_stage_a(l, t, half)
            pool_stage_b(l, t, half)
            pool_stage_c(l, t, half)

        def pre_work(gi, li, t):
            l, half = layers[li], gi % 2
            P.prio = 1
            if l % 2 == 0:
                pool_tile(l, t, half)
                ffn_pre_tile(l, t, half)
            else:
                sconv_pre_tile(l, t, half)
            P.prio = 0

        if INTERLEAVE:
            for tb in range(NBLK):
                load_block(0, tb)
            for t in range(NT):
                pre_work(0, 0, t)
            for gi in range(ngroups):
                half = gi % 2
                stored = 0
                pre_done = 0
                for li, l in enumerate(layers):
                    if l % 2 == 1:
                        sconv_s1(l)
                        for t in range(NT):
                            sconv_s2_tile(l, t)
                            ffn_pre_tile(l, t, half)
                    ffn_u(l)
                    for t in range(NT):
                        ffn_d_tile(l, t)
                        if li + 1 < len(layers):
                            pre_work(gi, li + 1, t)
                        else:
                            P.prio = 1
                            while stored < NBLK and max(tiles_of(stored)) <= t:
                                store_block(gi, stored)
                                if gi + 1 < ngroups:
                                    load_block(gi + 1, stored)
                                stored += 1
                            P.prio = 0
                            if gi + 1 < ngroups:
                                while pre_done < NT and (pre_done + 1) * N <= min(GRP, stored * 128):
                                    pre_work(gi + 1, 0, pre_done)
                                    pre_done += 1
        else:
            for gi in range(ngroups):
                half = gi % 2
                for tb in range(NBLK):
                    load_block(gi, tb)
                for li, l in enumerate(layers):
                    if l % 2 == 0:
                        ffn_prefetch(l)
                        pool_phase(l, half)
                    else:
                        for t in range(NT):
                            sconv_pre_tile(l, t, half)
                        sconv_s1(l)
                        ffn_prefetch(l)
                        for t in range(NT):
                            sconv_s2_tile(l, t)
                        for t in range(NT):
                            ffn_pre_tile(l, t, half)
                    ffn_u(l)
                    if li + 1 < len(layers) and layers[li + 1] % 2 == 1:
                        sconv_prefetch(layers[li + 1])
                    for t in range(NT):
                        ffn_d_tile(l, t)
                for tb in range(NBLK):
                    store_block(gi, tb)

        if SCHEDULE:
            P.schedule()
        P.emit(nc, final_wait_ops=list(out_dmas.values()))
    return nc


def _host_tables(pool_scale, sc_conv, ffn_conv, norm_g):
    par = np.zeros((128, NPAR), np.float32)
    par[:, PC_G:PC_G + 128] = np.asarray(norm_g, np.float32).reshape(DEPTH * 4, KC, 128).transpose(2, 0, 1).reshape(128, 128)
    par[:, PC_PS:PC_PS + 16] = np.asarray(pool_scale, np.float32).reshape(2, KC, 128).transpose(2, 0, 1).reshape(128, 16)
    par[:, PC_SC:PC_SC + 48] = np.asarray(sc_conv, np.float32).reshape(2 * 3, KC, 128).transpose(2, 0, 1).reshape(128, 48)
    par[:, PC_FC:PC_FC + DEPTH * 3 * JC] = np.asarray(ffn_conv, np.float32).reshape(DEPTH * 3, JC, 128).transpose(2, 0, 1).reshape(128, DEPTH * 3 * JC)
    cnt = np.zeros((128, 4, 16), np.float32)
    for g, w in enumerate(POOL_W):
        cnt[:, g, :] = 1.0 / np.minimum(np.arange(16) + 1.0, float(w))
    ident = np.eye(128, dtype=np.float32)
    return par, cnt.reshape(128, 64), ident


_NC_CACHE = {}


def _get_nc(layers, ngroups):
    key = (tuple(layers), ngroups)
    if key not in _NC_CACHE:
        _NC_CACHE[key] = build(layers, ngroups)
    return _NC_CACHE[key]


LAUNCH_PLAN = [(0, 1, 2, 3)]


def kernel(x, meta_tokens, pool_w, pool_scale, sc_w_in, sc_conv, sc_w_out,
           ffn_w_up, ffn_conv, ffn_w_down, norm_g):
    x = np.asarray(x, np.float32)
    B = x.shape[0]
    ncores = 8
    meta = np.broadcast_to(np.asarray(meta_tokens, np.float32)[None], (B, NMETA, D))
    hcur = np.ascontiguousarray(np.concatenate([meta, x], axis=1))
    par, cnt, ident = _host_tables(pool_scale, sc_conv, ffn_conv, norm_g)
    common = {
        "params": par, "invcnt": cnt, "ident": ident,
        "pool_w": np.ascontiguousarray(pool_w, np.float32),
        "sc_w_in": np.ascontiguousarray(sc_w_in, np.float32),
        "sc_w_out": np.ascontiguousarray(sc_w_out, np.float32),
        "ffn_w_up": np.ascontiguousarray(ffn_w_up, np.float32),
        "ffn_w_down": np.ascontiguousarray(ffn_w_down, np.float32),
    }
    per = B // ncores
    for layers in LAUNCH_PLAN:
        nc = _get_nc(layers, 2 * per)
        in_maps = []
        for c in range(ncores):
            m = dict(common)
            m["hin"] = np.ascontiguousarray(hcur[c * per:(c + 1) * per])
            in_maps.append(m)
        res = run_bass_kernel_spmd(nc, in_maps, core_ids=list(range(ncores)))
        hcur = np.concatenate([np.asarray(r["out"], np.float32) for r in res.results], axis=0)
    return np.ascontiguousarray(hcur[:, NMETA:, :])
```

```python
import contextlib
import numpy as np
import concourse.bass as bass
import concourse.mybir as mybir
from concourse.bass_utils import run_bass_kernel_spmd

F32 = mybir.dt.float32
BF16 = mybir.dt.bfloat16
ALU = mybir.AluOpType
AF = mybir.ActivationFunctionType

ENGS = ("pe", "act", "dve", "pool", "sp")

D = 1024
KC = 8
DFF = 2816
JC = 22
NMETA = 16
SEQ = 2048
TSEQ = SEQ + NMETA
GRP = TSEQ // 2
NT = 3
N = GRP // NT
HP = 15
HC = 2
DEPTH = 4
EPS = 1e-6
SCHEDULE = True
INTERLEAVE = False
MMC = 0.155
POOL_W = (2, 4, 8, 16)

PC_G = 0
PC_PS = 128
PC_SC = 144
PC_FC = 192
NPAR = 192 + DEPTH * 3 * JC


def I(name, *args, **kw):
    return lambda e: getattr(e, name)(*args, **kw)


class Op:
    __slots__ = ("idx", "eng", "fns", "deps", "dkey", "signal", "semval", "sem", "cost", "nbytes", "prio")

    def __init__(self, idx, eng, fns, deps, dkey, cost, nbytes):
        self.idx = idx
        self.eng = eng
        self.fns = fns
        self.deps = deps
        self.dkey = dkey
        self.signal = False
        self.semval = 0
        self.sem = None
        self.cost = cost
        self.nbytes = nbytes
        self.prio = 0


SCHED_WINDOW = {"pe": 8, "act": 8, "dve": 8, "pool": 10, "sp": 6}


class Prog:
    def __init__(self):
        self.ops = []
        self.last_w = {}
        self.readers = {}
        self.last_dma = {}
        self.order = None
        self.prio = 0

    def op(self, eng, fn, reads=(), writes=(), dkey=None, cost=0.5, nbytes=0):
        deps = set()
        for r in reads:
            w = self.last_w.get(r)
            if w is not None:
                deps.add(w)
        for r in writes:
            w = self.last_w.get(r)
            if w is not None:
                deps.add(w)
            for q in self.readers.get(r, ()):
                deps.add(q)
        if dkey is not None:
            p = self.last_dma.get(dkey)
            if p is not None:
                deps.add(p)
        idx = len(self.ops)
        fns = fn if isinstance(fn, (list, tuple)) else [fn]
        o = Op(idx, eng, fns, deps, dkey, cost, nbytes)
        o.prio = self.prio
        self.ops.append(o)
        for r in writes:
            self.last_w[r] = idx
            self.readers[r] = []
        for r in reads:
            if r not in writes:
                self.readers.setdefault(r, []).append(idx)
        if dkey is not None:
            self.last_dma[dkey] = idx
        return idx

    def alias(self, old_keys, new_keys):
        s = set()
        for k in old_keys:
            w = self.last_w.pop(k, None)
            if w is not None:
                s.add(w)
            s.update(self.readers.pop(k, ()))
        for k in new_keys:
            self.readers.setdefault(k, []).extend(s)

    def schedule(self):
        ops = self.ops
        n = len(ops)
        lists = {e: [o.idx for o in ops if o.eng == e] for e in ENGS}
        head = {e: 0 for e in ENGS}
        tfree = {e: 0.0 for e in ENGS}
        done = [False] * n
        finish = [0.0] * n
        order = {e: [] for e in ENGS}
        dma_t = 0.0
        remaining = n
        while remaining:
            best = None
            for e in ENGS:
                lst = lists[e]
                hd = head[e]
                while hd < len(lst) and done[lst[hd]]:
                    hd += 1
                head[e] = hd
                cnt = 0
                i = hd
                tf = tfree[e]
                W = SCHED_WINDOW[e]
                cand = None
                while i < len(lst) and cnt < W:
                    k = lst[i]
                    i += 1
                    if done[k]:
                        continue
                    cnt += 1
                    r = 0.0
                    ok = True
                    for d in ops[k].deps:
                        if not done[d]:
                            ok = False
                            break
                        f = finish[d]
                        if f > r:
                            r = f
                    if not ok:
                        continue
                    pr = ops[k].prio
                    if r <= tf:
                        key = (0, pr, k)
                    else:
                        key = (1, r, pr, k)
                    if cand is None or key < cand[0]:
                        cand = (key, r if r > tf else tf, k)
                    if r <= tf and pr == 0:
                        break
                if cand is not None:
                    st, k = cand[1], cand[2]
                    if best is None or st < best[0] or (st == best[0] and k < best[1]):
                        best = (st, k, e)
            st, k, e = best
            o = ops[k]
            if o.dkey is not None:
                tfree[e] = st + o.cost
                dma_t = max(dma_t, st + o.cost) + o.nbytes / 330e3
                finish[k] = dma_t + 2.0
            else:
                tfree[e] = st + o.cost
                finish[k] = st + o.cost + (0.25 if e == "pe" else 0.15)
            done[k] = True
            order[e].append(k)
            remaining -= 1
        self.order = order
        return max(finish) if n else 0.0

    def emit(self, nc, final_wait_ops=()):
        ops = self.ops
        order = self.order or {e: [o.idx for o in ops if o.eng == e] for e in ENGS}
        sem_deps = {}
        for o in ops:
            if o.eng == "pe" and o.dkey is None:
                sd = [d for d in o.deps if not (ops[d].eng == "pe" and ops[d].dkey is None)]
            else:
                sd = list(o.deps)
            sem_deps[o.idx] = sd
            for d in sd:
                ops[d].signal = True
        for d in final_wait_ops:
            ops[d].signal = True
        for o in ops:
            if o.dkey is not None:
                o.signal = True
        dkeys = sorted({o.dkey for o in ops if o.dkey is not None})
        with contextlib.ExitStack() as es:
            esem = {e: es.enter_context(nc.semaphore("s_" + e)) for e in ENGS}
            dsem = {k: es.enter_context(nc.semaphore("d_" + k)) for k in dkeys}
            dcount = {k: 0 for k in dkeys}
            for o in ops:
                if o.dkey is not None:
                    dcount[o.dkey] += 16
                    o.sem = dsem[o.dkey]
                    o.semval = dcount[o.dkey]
            for e in ENGS:
                c = 0
                for k in order[e]:
                    o = ops[k]
                    if o.signal and o.dkey is None:
                        c += 1
                        o.sem = esem[e]
                        o.semval = c
            block = es.enter_context(nc.Block())
            fin = [ops[d] for d in final_wait_ops]

            def make(ename, with_final):
                def body(eng):
                    waited = {}
                    for k in order[ename]:
                        o = ops[k]
                        need = {}
                        for d in sem_deps[k]:
                            p = ops[d]
                            kk = id(p.sem)
                            if p.semval > need.get(kk, (0, None))[0]:
                                need[kk] = (p.semval, p.sem)
                        for kk, (v, sm) in need.items():
                            if waited.get(kk, 0) >= v:
                                continue
                            eng.wait_ge(sm, v)
                            waited[kk] = v
                        ins = None
                        for fn in o.fns:
                            ins = fn(eng)
                        if o.signal:
                            ins.then_inc(o.sem, 16 if o.dkey is not None else 1)
                    if with_final:
                        for p in fin:
                            eng.wait_ge(p.sem, p.semval)
                return body

            block.tensor(make("pe", False))
            block.scalar(make("act", False))
            block.vector(make("dve", False))
            block.gpsimd(make("pool", False))
            block.sync(make("sp", True))


def build(layers=(0, 1, 2, 3), ngroups=4):
    nc = bass.Bass("TRN2", target_bir_lowering=False)
    nseq = (ngroups + 1) // 2
    hin = nc.dram_tensor("hin", [nseq, TSEQ, D], F32, kind="ExternalInput").ap()
    par_d = nc.dram_tensor("params", [128, NPAR], F32, kind="ExternalInput").ap()
    cnt_d = nc.dram_tensor("invcnt", [128, 4 * 16], F32, kind="ExternalInput").ap()
    idn_d = nc.dram_tensor("ident", [128, 128], F32, kind="ExternalInput").ap()
    pool_w = nc.dram_tensor("pool_w", [2, 4, 256, 256], F32, kind="ExternalInput").ap()
    sc_w_in = nc.dram_tensor("sc_w_in", [2, D, 3 * D], F32, kind="ExternalInput").ap()
    sc_w_out = nc.dram_tensor("sc_w_out", [2, D, D], F32, kind="ExternalInput").ap()
    w_up = nc.dram_tensor("ffn_w_up", [DEPTH, D, 2 * DFF], F32, kind="ExternalInput").ap()
    w_down = nc.dram_tensor("ffn_w_down", [DEPTH, DFF, D], F32, kind="ExternalInput").ap()
    out = nc.dram_tensor("out", [nseq, TSEQ, D], F32, kind="ExternalOutput").ap()

    P = Prog()
    es = contextlib.ExitStack()
    with es:
        def sb(name, shape, dt):
            return es.enter_context(nc.sbuf_tensor(name, shape, dt))

        h = sb("h", [128, KC, GRP], F32)
        actbuf = sb("actbuf", [128, JC * GRP // 2], F32)
        ugrp = sb("ugrp", [128, KC, GRP + HC], BF16)
        wa = sb("wa", [128, JC, D], BF16)
        ring = [sb(f"ring{i}", [128, 2 * KC * 384], BF16) for i in range(2)]
        sq = sb("sq", [128, KC, N + HC], BF16)
        NVEC = 16
        vbuf = sb("vbuf", [128, NVEC, N + HC], F32)
        vecs = [vbuf[:, i, :] for i in range(NVEC)]
        par = sb("par", [128, NPAR], F32)
        cnt = sb("cnt", [128, 4, 16], F32)
        idn = sb("idn", [128, 128], F32)
        ones = sb("ones", [128, 128], BF16)
        stg = [sb(f"stg{i}", [128, D], F32) for i in range(2)]
        halo_f = {l: sb(f"halof{l}", [128, KC, HC], BF16) for l in range(DEPTH)}
        halo_m = {}
        for l in range(DEPTH):
            if l % 2 == 0:
                halo_m[l] = sb(f"halom{l}", [128, KC, HP], F32)
            else:
                halo_m[l] = sb(f"halom{l}", [128, KC, HC], BF16)
        ps = [es.enter_context(nc.psum_tensor(f"ps{i}", [128, 512], F32)) for i in range(8)]

        ab16 = actbuf[:, :].bitcast(BF16)
        RW16 = JC * N
        RW32 = RW16 // 2
        act_t = [ab16[:, t * RW16:(t + 1) * RW16].rearrange("p (j n) -> p j n", j=JC) for t in range(NT)]
        y_t = [ab16[:, t * RW16:t * RW16 + KC * N].rearrange("p (j n) -> p j n", j=KC) for t in range(NT)]
        PW = 2 * (N + HP)
        assert 4 * PW <= RW32
        pscr = [[actbuf[:, t * RW32 + i * PW:t * RW32 + (i + 1) * PW].rearrange("p (c n) -> p c n", c=2)
                 for i in range(4)] for t in range(NT)]
        reg_keys = [{"act": [f"act{t}_{j}" for j in range(JC)],
                     "y": [f"y{t}_{m}" for m in range(KC)],
                     "pool": [f"pscr{t}_{i}" for i in range(4)]} for t in range(NT)]
        reg_mode = ["act"] * NT

        def region_to(t, mode):
            if reg_mode[t] != mode:
                P.alias(reg_keys[t][reg_mode[t]], reg_keys[t][mode])
                reg_mode[t] = mode

        cnts = {"psA": 0, "psB": 0, "psC": 0, "psC6": 0, "psS": 0, "A3": 0, "B3": 0, "S1c": 0, "S1v": 0, "S1b": 0, "vec": 0, "ring": 0, "stg": 0, "alt": 0}

        def g_ap(l, i, c):
            col = PC_G + (l * 4 + i) * 8 + c
            return par[:, col:col + 1]

        def ps_ap(j, c):
            col = PC_PS + j * 8 + c
            return par[:, col:col + 1]

        def sc_ap(j, k, c):
            col = PC_SC + (j * 3 + k) * 8 + c
            return par[:, col:col + 1]

        def fc_ap(l, k, jc):
            col = PC_FC + (l * 3 + k) * JC + jc
            return par[:, col:col + 1]

        P.op("sp", I("dma_start", out=par[:], in_=par_d), writes=["par"], dkey="par")
        P.op("sp", I("dma_start", out=cnt[:].rearrange("p g s -> p (g s)"), in_=cnt_d), writes=["cnt"], dkey="cnt")
        P.op("sp", I("dma_start", out=idn[:], in_=idn_d), writes=["idn"], dkey="idn")
        P.op("dve", I("memset", ones[:], 1.0), writes=["ones"])
        epsb = sb("epsb", [128, 1], F32)
        eps_ap = epsb[:, 0:1]
        P.op("dve", I("memset", epsb[:], EPS), writes=["eps"])

        def bank(role):
            if role in ("S", "T"):
                i = 6 + (cnts["psS"] % 2)
                cnts["psS"] += 1
                return i, f"ps{i}"
            if role == "C6":
                i = (4, 5, 0, 1, 2, 3)[cnts["psC6"] % 6]
                cnts["psC6"] += 1
                return i, f"ps{i}"
            if role in ("S1c", "S1v", "S1b"):
                bl = {"S1c": (0, 1), "S1v": (2,), "S1b": (3, 4, 5)}[role]
                i = bl[cnts[role] % len(bl)]
                cnts[role] += 1
                return i, f"ps{i}"
            if role in ("A3", "B3"):
                i = {"A3": (0, 1, 4), "B3": (2, 3, 5)}[role][cnts[role] % 3]
                cnts[role] += 1
                return i, f"ps{i}"
            base = {"A": 0, "B": 2, "C": 4}[role]
            k = "ps" + role
            i = base + (cnts[k] % 2)
            cnts[k] += 1
            return i, f"ps{i}"

        def vec():
            i = cnts["vec"] % NVEC
            cnts["vec"] += 1
            return vecs[i], f"vec{i}"

        def ring_slot():
            i = cnts["ring"] % 2
            cnts["ring"] += 1
            return ring[i], f"ring{i}"

        def load_wup_slot(l, sg):
            j0 = 3 * sg
            nj = min(3, JC - j0)
            slot, key = ring_slot()
            sv = slot[:, 0:2 * KC * 384].rearrange("p (g k c) -> p g k c", g=2, k=KC)
            for gv in range(2):
                c0 = gv * DFF + j0 * 128
                src = w_up[l, :, c0:c0 + nj * 128].rearrange("(k p) c -> p k c", p=128)
                dst = sv[:, gv, :, 0:nj * 128]
                P.op("pool", I("dma_start", out=dst, in_=src),
                     writes=[f"{key}_b{u}" for u in range(3 * gv, 3 * gv + 3)], dkey=key + f"_{gv}", cost=1.0,
                     nbytes=nj * 128 * D * 4)
            return sv, key, j0, nj

        def load_wa_down(l):
            bounds = [0, 6, 12, 17, 22]
            for pi in range(4):
                a, b = bounds[pi], bounds[pi + 1]
                src = w_down[l, a * 128:b * 128, :].rearrange("(j p) c -> p j c", p=128)
                dst = wa[:, a:b, :]
                P.op("pool", I("dma_start", out=dst, in_=src),
                     writes=[f"wa{pi}"], dkey=f"wa{pi}", cost=1.0, nbytes=(b - a) * 128 * D * 4)

        def load_wa_out(jl):
            for pi, (a, b) in enumerate(((0, 6), (6, 8))):
                src = sc_w_out[jl, a * 128:b * 128, :].rearrange("(j p) c -> p j c", p=128)
                dst = wa[:, a:b, :]
                P.op("pool", I("dma_start", out=dst, in_=src),
                     writes=[f"wa{pi}"], dkey=f"wa{pi}", cost=1.0, nbytes=(b - a) * 128 * D * 4)

        def load_win_slot(jl, sg):
            slot, key = ring_slot()
            sv = slot[:, 0:3 * KC * 256].rearrange("p (g k c) -> p g k c", g=3, k=KC)
            for gi in range(3):
                c0 = gi * D + sg * 256
                src = sc_w_in[jl, :, c0:c0 + 256].rearrange("(k p) c -> p k c", p=128)
                dst = sv[:, gi, :, :]
                P.op("pool", I("dma_start", out=dst, in_=src),
                     writes=[f"{key}_b{u}" for u in range(2 * gi, 2 * gi + 2)], dkey=key + f"_{gi}", cost=1.0,
                     nbytes=256 * D * 4)
            return sv, key

        pwres = sb("pwres", [128, 2, 4, 2, 256], BF16)
        for jl in range(2):
            for g in range(4):
                P.op("pool", I("dma_start", out=pwres[:, jl, g, :, :], in_=pool_w[jl, g].rearrange("(k p) c -> p k c", p=128)),
                     writes=[f"pw{jl}"], dkey=f"pw{jl}", cost=1.0, nbytes=256 * 256 * 4)

        def rstd_from_ss(sbank, skey, ncols):
            t1, k1 = vec()
            P.op("act", I("activation", out=t1[:, 0:ncols], in_=ps[sbank][:, 0:ncols], func=AF.Ln,
                          bias=eps_ap, scale=1.0 / D),
                 reads=[skey, "eps"], writes=[k1], cost=1.9)
            P.op("act", I("activation", out=t1[:, 0:ncols], in_=t1[:, 0:ncols], func=AF.Exp, scale=-0.5),
                 reads=[k1], writes=[k1], cost=1.9)
            return t1, k1

        def hk(t):
            return f"h{t}"

        sq_mode = ["whole"]

        def sq_to(mode):
            if sq_mode[0] == mode:
                return
            if mode == "chunks":
                P.alias(["sq"], [f"sqm{m}" for m in range(KC)])
            else:
                P.alias([f"sqm{m}" for m in range(KC)], ["sq"])
            sq_mode[0] = mode

        def norm_stats(t):
            o = t * N
            sq_to("whole")
            P.op("act", I("activation", out=sq[:, :, 0:N], in_=h[:, :, o:o + N], func=AF.Square),
                 reads=[hk(t)], writes=["sq"], cost=2.2)
            sbank, skey = bank("S")
            P.op("pe", [I("matmul", ps[sbank][:, 0:N], lhsT=ones[:], rhs=sq[:, c, 0:N],
                          start=(c == 0), stop=(c == KC - 1)) for c in range(KC)],
                 reads=["sq", "ones"], writes=[skey], cost=KC * MMC)
            return rstd_from_ss(sbank, skey, N)

        def prenorm_ugrp(l, i, t):
            o = t * N
            r, kr = norm_stats(t)
            for c in range(KC):
                P.op("dve", I("scalar_tensor_tensor", out=ugrp[:, c, HC + o:HC + o + N], in0=h[:, c, o:o + N],
                              scalar=g_ap(l, i, c), in1=r[:, 0:N], op0=ALU.mult, op1=ALU.mult),
                     reads=[hk(t), kr, "par"], writes=[f"ug{t}"], cost=0.5)

        class PostNorm:
            def __init__(self, l, i, t, scale_fn=None):
                self.l, self.i, self.t = l, i, t
                self.scale_fn = scale_fn
                sq_to("chunks")
                self.sbank, self.skey = bank("S")
                self.bufs = {}

            def chunk(self, m, pbank, pkey):
                sc = self.scale_fn(m) if self.scale_fn is not None else 1.0
                rd = [pkey] + (["par"] if self.scale_fn is not None else [])
                fr, kf = vec()
                self.bufs[m] = (fr, kf)
                P.op("act", I("activation", out=sq[:, m, 0:N], in_=ps[pbank][:, 0:N], func=AF.Square, scale=sc),
                     reads=rd, writes=[f"sqm{m}"], cost=0.45)
                P.op("act", I("activation", out=fr[:, 0:N], in_=ps[pbank][:, 0:N], func=AF.Copy, scale=sc),
                     reads=rd, writes=[kf], cost=0.55)
                P.op("pe", I("matmul", ps[self.sbank][:, 0:N], lhsT=ones[:], rhs=sq[:, m, 0:N],
                             start=(m == 0), stop=(m == KC - 1)),
                     reads=[f"sqm{m}", "ones"], writes=[self.skey], cost=MMC)

            def finish(self):
                l, i, t = self.l, self.i, self.t
                o = t * N
                r, kr = rstd_from_ss(self.sbank, self.skey, N)
                for m in range(KC):
                    fr, kf = self.bufs[m]
                    P.op("dve", I("scalar_tensor_tensor", out=fr[:, 0:N], in0=fr[:, 0:N], scalar=g_ap(l, i, m),
                                  in1=r[:, 0:N], op0=ALU.mult, op1=ALU.mult),
                         reads=[kf, kr, "par"], writes=[kf], cost=0.58)
                    P.op("pool", I("tensor_tensor", out=h[:, m, o:o + N], in0=h[:, m, o:o + N], in1=fr[:, 0:N], op=ALU.add),
                         reads=[kf, hk(t)], writes=[hk(t)], cost=0.95)

        NBLK = (GRP + 127) // 128

        def blk(tb):
            r0 = tb * 128
            return r0, min(128, GRP - r0)

        def tiles_of(tb):
            r0, nr = blk(tb)
            return sorted({min(NT - 1, r // N) for r in (r0, r0 + nr - 1)})

        def load_block(gi, tb):
            b, half = gi // 2, gi % 2
            t0 = half * GRP
            r0, nr = blk(tb)
            si = cnts["stg"] % 2
            cnts["stg"] += 1
            s = stg[si]
            P.op("sp", I("dma_start", out=s[0:nr, :], in_=hin[b, t0 + r0:t0 + r0 + nr, :]),
                 writes=[f"stg{si}"], dkey=f"ld{si}", cost=0.1, nbytes=nr * D * 4)
            hkeys = [hk(t) for t in tiles_of(tb)]
            for hf in range(2):
                tb_i, tkey = bank("T")
                pv = ps[tb_i][:, 0:4 * 128].rearrange("p (c t) -> p c t", c=4)
                P.op("pe", [I("transpose", out=pv[:, cc, 0:nr], in_=s[0:nr, (hf * 4 + cc) * 128:(hf * 4 + cc + 1) * 128],
                              identity=idn[0:nr, 0:nr]) for cc in range(4)],
                     reads=[f"stg{si}", "idn"], writes=[tkey], cost=1.2)
                P.op("act", I("activation", out=h[:, hf * 4:hf * 4 + 4, r0:r0 + nr], in_=pv[:, :, 0:nr], func=AF.Copy),
                     reads=[tkey], writes=hkeys, cost=0.6)

        out_dmas = {}

        def store_block(gi, tb):
            b, half = gi // 2, gi % 2
            t0 = half * GRP
            r0, nr = blk(tb)
            si = cnts["stg"] % 2
            cnts["stg"] += 1
            s = stg[si]
            hkeys = [hk(t) for t in tiles_of(tb)]
            for hf in range(2):
                tb_i, tkey = bank("T")
                P.op("pe", [I("transpose", out=ps[tb_i][0:nr, cc * 128:(cc + 1) * 128], in_=h[:, hf * 4 + cc, r0:r0 + nr],
                              identity=idn[:]) for cc in range(4)],
                     reads=hkeys + ["idn"], writes=[tkey], cost=1.2)
                P.op("act", I("activation", out=s[0:nr, hf * 512:(hf + 1) * 512], in_=ps[tb_i][0:nr, 0:512], func=AF.Copy),
                     reads=[tkey], writes=[f"stg{si}"], cost=0.6)
            d = P.op("sp", I("dma_start", out=out[b, t0 + r0:t0 + r0 + nr, :], in_=s[0:nr, :]),
                     reads=[f"stg{si}"], dkey=f"st{si}", cost=0.1, nbytes=nr * D * 4)
            out_dmas[f"st{si}"] = d

        def ugrp_halo_fill(src, srckey, half):
            if half == 0:
                P.op("dve", I("memset", ugrp[:, :, 0:HC], 0.0), writes=["ugh"], cost=0.1)
            else:
                P.op("dve", I("tensor_copy", out=ugrp[:, :, 0:HC], in_=src[:, :, :]),
                     reads=[srckey], writes=["ugh"], cost=0.1)

        def ugrp_halo_save(dst, dstkey):
            P.op("dve", I("tensor_copy", out=dst[:, :, :], in_=ugrp[:, :, GRP:GRP + HC]),
                 reads=[f"ug{NT - 1}"], writes=[dstkey], cost=0.1)

        def ffn_pre_tile(l, t, half):
            if t == 0:
                ugrp_halo_fill(halo_f[l], f"halof{l}", half)
            prenorm_ugrp(l, 2, t)
            if t == NT - 1:
                ugrp_halo_save(halo_f[l], f"halof{l}")

        def sconv_pre_tile(l, t, half):
            if t == 0:
                ugrp_halo_fill(halo_m[l], f"halom{l}", half)
            prenorm_ugrp(l, 0, t)
            if t == NT - 1:
                ugrp_halo_save(halo_m[l], f"halom{l}")

        def ukeys_of(t):
            return [f"ug{t}"] + ([f"ug{t - 1}"] if t > 0 else ["ugh"])

        NSG = (JC + 2) // 3
        pre_wup = {}
        pre_win = {}

        def ffn_prefetch(l):
            pre_wup[l] = [load_wup_slot(l, 0), load_wup_slot(l, 1)]

        def sconv_prefetch(l):
            pre_win[l] = [load_win_slot(l // 2, 0), load_win_slot(l // 2, 1)]

        def ffn_u(l):
            for t in range(NT):
                region_to(t, "act")
            load_wa_down(l)
            if l not in pre_wup:
                ffn_prefetch(l)
            slots = pre_wup.pop(l)
            nsg = NSG
            for sg in range(nsg):
                sv, key, j0, nj = slots[sg]
                for t in range(NT):
                    o = t * N
                    for jj in range(nj):
                        j = j0 + jj
                        ga, gk = bank("A3")
                        va, vk = bank("B3")
                        P.op("pe", [I("matmul", ps[ga][:, 0:N + HC], lhsT=sv[:, 0, k, jj * 128:(jj + 1) * 128],
                                      rhs=ugrp[:, k, o:o + N + HC], start=(k == 0), stop=(k == KC - 1)) for k in range(KC)],
                             reads=ukeys_of(t) + [f"{key}_b{u}" for u in range(0, 3)], writes=[gk], cost=KC * MMC)
                        P.op("pe", [I("matmul", ps[va][:, 0:N], lhsT=sv[:, 1, k, jj * 128:(jj + 1) * 128],
                                      rhs=ugrp[:, k, o + HC:o + HC + N], start=(k == 0), stop=(k == KC - 1)) for k in range(KC)],
                             reads=[f"ug{t}"] + [f"{key}_b{u}" for u in range(3, 6)], writes=[vk], cost=KC * MMC)
                        a, ka = vec()
                        P.op("act", I("activation", out=a[:, 0:N], in_=ps[ga][:, 2:N + 2], func=AF.Copy, scale=fc_ap(l, 2, j)),
                             reads=[gk, "par"], writes=[ka], cost=0.5)
                        P.op("dve", I("scalar_tensor_tensor", out=a[:, 0:N], in0=ps[ga][:, 1:N + 1], scalar=fc_ap(l, 1, j),
                                      in1=a[:, 0:N], op0=ALU.mult, op1=ALU.add),
                             reads=[gk, "par", ka], writes=[ka], cost=0.5)
                        P.op("dve", I("scalar_tensor_tensor", out=a[:, 0:N], in0=ps[ga][:, 0:N], scalar=fc_ap(l, 0, j),
                                      in1=a[:, 0:N], op0=ALU.mult, op1=ALU.add),
                             reads=[gk, "par", ka], writes=[ka], cost=0.5)
                        s_, ks = vec()
                        P.op("act", I("activation", out=s_[:, 0:N], in_=a[:, 0:N], func=AF.Silu),
                             reads=[ka], writes=[ks], cost=0.5)
                        P.op("dve", I("tensor_tensor", out=act_t[t][:, j, :], in0=s_[:, 0:N], in1=ps[va][:, 0:N], op=ALU.mult),
                             reads=[ks, vk], writes=[f"act{t}_{j}"], cost=0.5)
                if sg + 2 < nsg:
                    slots.append(load_wup_slot(l, sg + 2))

        def ffn_d_tile(l, t):
            pn = PostNorm(l, 3, t)
            for m in range(KC):
                fb, fk = bank("C6")
                P.op("pe", [I("matmul", ps[fb][:, 0:N], lhsT=wa[:, j, m * 128:(m + 1) * 128], rhs=act_t[t][:, j, :],
                              start=(j == 0), stop=(j == JC - 1)) for j in range(JC)],
                     reads=[f"act{t}_{j}" for j in range(JC)] + ["wa0", "wa1", "wa2", "wa3"], writes=[fk], cost=JC * MMC)
                pn.chunk(m, fb, fk)
            pn.finish()

        def sconv_s1(l):
            jl = l // 2
            for t in range(NT):
                region_to(t, "y")
            if l not in pre_win:
                sconv_prefetch(l)
            slots = pre_win.pop(l)
            load_wa_out(jl)
            for sg in range(4):
                sv, key = slots[sg]
                for t in range(NT):
                    o = t * N
                    for mm in range(2):
                        m = 2 * sg + mm
                        ca, ck = bank("S1c")
                        va, vk = bank("S1v")
                        ba, bk = bank("S1b")
                        for gi, (bb, ncol, off) in enumerate(((ba, N, HC), (ca, N + HC, 0), (va, N + HC, 0))):
                            P.op("pe", [I("matmul", ps[bb][:, 0:ncol], lhsT=sv[:, gi, k, mm * 128:(mm + 1) * 128],
                                          rhs=ugrp[:, k, o + off:o + off + ncol], start=(k == 0), stop=(k == KC - 1))
                                        for k in range(KC)],
                                 reads=ukeys_of(t) + [f"{key}_b{u}" for u in range(2 * gi, 2 * gi + 2)],
                                 writes=[(bk, ck, vk)[gi]], cost=KC * MMC)
                        vs, kvs = vec()
                        P.op("act", I("activation", out=vs[:, 0:N + HC], in_=ps[va][:, 0:N + HC], func=AF.Copy),
                             reads=[vk], writes=[kvs], cost=0.5)
                        pr, kp = vec()
                        P.op("dve", I("tensor_tensor", out=pr[:, 0:N + HC], in0=vs[:, 0:N + HC], in1=ps[ca][:, 0:N + HC], op=ALU.mult),
                             reads=[kvs, ck], writes=[kp], cost=0.5)
                        a, ka = vec()
                        P.op("act", I("activation", out=a[:, 0:N], in_=pr[:, 2:N + 2], func=AF.Copy, scale=sc_ap(jl, 2, m)),
                             reads=[kp, "par"], writes=[ka], cost=0.5)
                        P.op("dve", I("scalar_tensor_tensor", out=a[:, 0:N], in0=pr[:, 1:N + 1], scalar=sc_ap(jl, 1, m),
                                      in1=a[:, 0:N], op0=ALU.mult, op1=ALU.add),
                             reads=[kp, "par", ka], writes=[ka], cost=0.5)
                        P.op("dve", I("scalar_tensor_tensor", out=a[:, 0:N], in0=pr[:, 0:N], scalar=sc_ap(jl, 0, m),
                                      in1=a[:, 0:N], op0=ALU.mult, op1=ALU.add),
                             reads=[kp, "par", ka], writes=[ka], cost=0.5)
                        P.op("dve", I("tensor_tensor", out=y_t[t][:, m, :], in0=a[:, 0:N], in1=ps[ba][:, 0:N], op=ALU.mult),
                             reads=[ka, bk], writes=[f"y{t}_{m}"], cost=0.5)
                if sg + 2 < 4:
                    slots.append(load_win_slot(jl, sg + 2))

        def sconv_s2_tile(l, t):
            pn = PostNorm(l, 1, t)
            for m in range(KC):
                fb, fk = bank("C6")
                P.op("pe", [I("matmul", ps[fb][:, 0:N], lhsT=wa[:, k, m * 128:(m + 1) * 128], rhs=y_t[t][:, k, :],
                              start=(k == 0), stop=(k == KC - 1)) for k in range(KC)],
                     reads=[f"y{t}_{k}" for k in range(KC)] + ["wa0", "wa1"], writes=[fk], cost=KC * MMC)
                pn.chunk(m, fb, fk)
            pn.finish()

        pool_r = {}

        def pool_stage_a(l, t, half):
            region_to(t, "pool")
            pool_r[t] = norm_stats(t)

        def pool_stage_b(l, t, half):
            o = t * N
            first = (half == 0 and t == 0)
            r, kr = pool_r[t]
            for g in range(4):
                W = POOL_W[g]
                pu = pscr[t][g % 2]
                pukey = f"pscr{t}_{g % 2}"
                hkey = f"halom{l}_{g}"
                if first:
                    P.op("dve", I("memset", pu[:, :, 0:HP], 0.0), writes=[pukey], cost=0.1)
                else:
                    P.op("dve", I("tensor_copy", out=pu[:, :, 0:HP], in_=halo_m[l][:, 2 * g:2 * g + 2, :]),
                         reads=[hkey], writes=[pukey], cost=0.1)
                for cc in range(2):
                    c = 2 * g + cc
                    P.op("dve", I("scalar_tensor_tensor", out=pu[:, cc, HP:HP + N], in0=h[:, c, o:o + N],
                                  scalar=g_ap(l, 0, c), in1=r[:, 0:N], op0=ALU.mult, op1=ALU.mult),
                         reads=[hk(t), kr, "par", pukey], writes=[pukey], cost=0.58)
                P.op("dve", I("tensor_copy", out=halo_m[l][:, 2 * g:2 * g + 2, :], in_=pu[:, :, N:N + HP]),
                     reads=[pukey], writes=[hkey], cost=0.1)
                cur, lo, curkey = pu, 0, pukey
                for s_ in range(g + 1):
                    sh = 1 << s_
                    nlo = HP - (W - 2 * sh)
                    ln = HP + N - nlo
                    dstb = pscr[t][2 + (s_ % 2)]
                    dkey_ = f"pscr{t}_{2 + (s_ % 2)}"
                    eng = "dve"
                    P.op(eng, I("tensor_tensor", out=dstb[:, :, 0:ln], in0=cur[:, :, nlo - lo:nlo - lo + ln],
                                in1=cur[:, :, nlo - lo - sh:nlo - lo - sh + ln], op=ALU.add),
                         reads=[curkey], writes=[dkey_], cost=0.9)
                    cur, lo, curkey = dstb, nlo, dkey_
                sfin = cur[:, :, HP - lo:HP - lo + N]
                pl = ugrp[:, 2 * g:2 * g + 2, HC + o:HC + o + N]
                P.op("dve", I("scalar_tensor_tensor", out=pl, in0=sfin, scalar=1.0 / W,
                              in1=pu[:, :, HP:HP + N], op0=ALU.mult, op1=ALU.subtract),
                     reads=[curkey, pukey], writes=[f"ug{t}"], cost=0.9)
                if first:
                    for cc in range(2):
                        tmpv, kt = vec()
                        P.op("dve", I("tensor_tensor", out=tmpv[:, 0:16], in0=sfin[:, cc, 0:16], in1=cnt[:, g, :], op=ALU.mult),
                             reads=[curkey, "cnt"], writes=[kt], cost=0.1)
                        P.op("dve", I("tensor_tensor", out=ugrp[:, 2 * g + cc, HC + o:HC + o + 16], in0=tmpv[:, 0:16],
                                      in1=pu[:, cc, HP:HP + 16], op=ALU.subtract),
                             reads=[kt, pukey], writes=[f"ug{t}"], cost=0.1)

        def pool_stage_c(l, t, half):
            jl = l // 2
            o = t * N
            pn = PostNorm(l, 1, t, scale_fn=lambda m: ps_ap(jl, m))
            for g in range(4):
                for dh in range(2):
                    fb, fk = bank("C6")
                    P.op("pe", [I("matmul", ps[fb][:, 0:N], lhsT=pwres[:, jl, g, kc, dh * 128:(dh + 1) * 128],
                                  rhs=ugrp[:, 2 * g + kc, HC + o:HC + o + N], start=(kc == 0), stop=(kc == 1)) for kc in range(2)],
                         reads=[f"ug{t}", f"pw{jl}"], writes=[fk], cost=2 * MMC)
                    pn.chunk(2 * g + dh, fb, fk)
            pn.finish()

        def pool_phase(l, half):
            stages = (pool_stage_a, pool_stage_b, pool_stage_c, ffn_pre_tile)
            for step in range(NT + len(stages) - 1):
                for k, st in enumerate(stages):
                    t = step - k
                    if 0 <= t < NT:
                        st(l, t, half)

        def pool_tile(l, t, half):
            pool_stage_a(l, t, half)
            pool_stage_b(l, t, half)
            pool_stage_c(l, t, half)

        def pre_work(gi, li, t):
            l, half = layers[li], gi % 2
            P.prio = 1
            if l % 2 == 0:
                pool_tile(l, t, half)
                ffn_pre_tile(l, t, half)
            else:
                sconv_pre_tile(l, t, half)
            P.prio = 0

        if INTERLEAVE:
            for tb in range(NBLK):
                load_block(0, tb)
            for t in range(NT):
                pre_work(0, 0, t)
            for gi in range(ngroups):
                half = gi % 2
                stored = 0
                pre_done = 0
                for li, l in enumerate(layers):
                    if l % 2 == 1:
                        sconv_s1(l)
                        for t in range(NT):
                            sconv_s2_tile(l, t)
                            ffn_pre_tile(l, t, half)
                    ffn_u(l)
                    for t in range(NT):
                        ffn_d_tile(l, t)
                        if li + 1 < len(layers):
                            pre_work(gi, li + 1, t)
                        else:
                            P.prio = 1
                            while stored < NBLK and max(tiles_of(stored)) <= t:
                                store_block(gi, stored)
                                if gi + 1 < ngroups:
                                    load_block(gi + 1, stored)
                                stored += 1
                            P.prio = 0
                            if gi + 1 < ngroups:
                                while pre_done < NT and (pre_done + 1) * N <= min(GRP, stored * 128):
                                    pre_work(gi + 1, 0, pre_done)
                                    pre_done += 1
        else:
            for gi in range(ngroups):
                half = gi % 2
                for tb in range(NBLK):
                    load_block(gi, tb)
                for li, l in enumerate(layers):
                    if l % 2 == 0:
                        ffn_prefetch(l)
                        pool_phase(l, half)
                    else:
                        for t in range(NT):
                            sconv_pre_tile(l, t, half)
                        sconv_s1(l)
                        ffn_prefetch(l)
                        for t in range(NT):
                            sconv_s2_tile(l, t)
                        for t in range(NT):
                            ffn_pre_tile(l, t, half)
                    ffn_u(l)
                    if li + 1 < len(layers) and layers[li + 1] % 2 == 1:
                        sconv_prefetch(layers[li + 1])
                    for t in range(NT):
                        ffn_d_tile(l, t)
                for tb in range(NBLK):
                    store_block(gi, tb)

        if SCHEDULE:
            P.schedule()
        P.emit(nc, final_wait_ops=list(out_dmas.values()))
    return nc


def _host_tables(pool_scale, sc_conv, ffn_conv, norm_g):
    par = np.zeros((128, NPAR), np.float32)
    par[:, PC_G:PC_G + 128] = np.asarray(norm_g, np.float32).reshape(DEPTH * 4, KC, 128).transpose(2, 0, 1).reshape(128, 128)
    par[:, PC_PS:PC_PS + 16] = np.asarray(pool_scale, np.float32).reshape(2, KC, 128).transpose(2, 0, 1).reshape(128, 16)
    par[:, PC_SC:PC_SC + 48] = np.asarray(sc_conv, np.float32).reshape(2 * 3, KC, 128).transpose(2, 0, 1).reshape(128, 48)
    par[:, PC_FC:PC_FC + DEPTH * 3 * JC] = np.asarray(ffn_conv, np.float32).reshape(DEPTH * 3, JC, 128).transpose(2, 0, 1).reshape(128, DEPTH * 3 * JC)
    cnt = np.zeros((128, 4, 16), np.float32)
    for g, w in enumerate(POOL_W):
        cnt[:, g, :] = 1.0 / np.minimum(np.arange(16) + 1.0, float(w))
    ident = np.eye(128, dtype=np.float32)
    return par, cnt.reshape(128, 64), ident


_NC_CACHE = {}


def _get_nc(layers, ngroups):
    key = (tuple(layers), ngroups)
    if key not in _NC_CACHE:
        _NC_CACHE[key] = build(layers, ngroups)
    return _NC_CACHE[key]


LAUNCH_PLAN = [(0, 1, 2, 3)]


def kernel(x, meta_tokens, pool_w, pool_scale, sc_w_in, sc_conv, sc_w_out,
           ffn_w_up, ffn_conv, ffn_w_down, norm_g):
    x = np.asarray(x, np.float32)
    B = x.shape[0]
    ncores = 8
    meta = np.broadcast_to(np.asarray(meta_tokens, np.float32)[None], (B, NMETA, D))
    hcur = np.ascontiguousarray(np.concatenate([meta, x], axis=1))
    par, cnt, ident = _host_tables(pool_scale, sc_conv, ffn_conv, norm_g)
    common = {
        "params": par, "invcnt": cnt, "ident": ident,
        "pool_w": np.ascontiguousarray(pool_w, np.float32),
        "sc_w_in": np.ascontiguousarray(sc_w_in, np.float32),
        "sc_w_out": np.ascontiguousarray(sc_w_out, np.float32),
        "ffn_w_up": np.ascontiguousarray(ffn_w_up, np.float32),
        "ffn_w_down": np.ascontiguousarray(ffn_w_down, np.float32),
    }
    per = B // ncores
    for layers in LAUNCH_PLAN:
        nc = _get_nc(layers, 2 * per)
        in_maps = []
        for c in range(ncores):
            m = dict(common)
            m["hin"] = np.ascontiguousarray(hcur[c * per:(c + 1) * per])
            in_maps.append(m)
        res = run_bass_kernel_spmd(nc, in_maps, core_ids=list(range(ncores)))
        hcur = np.concatenate([np.asarray(r["out"], np.float32) for r in res.results], axis=0)
    return np.ascontiguousarray(hcur[:, NMETA:, :])
```

```python
import contextlib
import numpy as np
import concourse.bass as bass
import concourse.mybir as mybir
from concourse.bass_utils import run_bass_kernel_spmd

F32 = mybir.dt.float32
BF16 = mybir.dt.bfloat16
ALU = mybir.AluOpType
AF = mybir.ActivationFunctionType

ENGS = ("pe", "act", "dve", "pool", "sp")

D = 1024
KC = 8
DFF = 2816
JC = 22
NMETA = 16
SEQ = 2048
TSEQ = SEQ + NMETA
GRP = TSEQ // 2
NT = 3
N = GRP // NT
HP = 15
HC = 2
DEPTH = 4
EPS = 1e-6
SCHEDULE = True
INTERLEAVE = False
MMC = 0.155
POOL_W = (2, 4, 8, 16)

PC_G = 0
PC_PS = 128
PC_SC = 144
PC_FC = 192
NPAR = 192 + DEPTH * 3 * JC


def I(name, *args, **kw):
    return lambda e: getattr(e, name)(*args, **kw)


class Op:
    __slots__ = ("idx", "eng", "fns", "deps", "dkey", "signal", "semval", "sem", "cost", "nbytes", "prio")

    def __init__(self, idx, eng, fns, deps, dkey, cost, nbytes):
        self.idx = idx
        self.eng = eng
        self.fns = fns
        self.deps = deps
        self.dkey = dkey
        self.signal = False
        self.semval = 0
        self.sem = None
        self.cost = cost
        self.nbytes = nbytes
        self.prio = 0


SCHED_WINDOW = {"pe": 8, "act": 8, "dve": 8, "pool": 10, "sp": 6}


class Prog:
    def __init__(self):
        self.ops = []
        self.last_w = {}
        self.readers = {}
        self.last_dma = {}
        self.order = None
        self.prio = 0

    def op(self, eng, fn, reads=(), writes=(), dkey=None, cost=0.5, nbytes=0):
        deps = set()
        for r in reads:
            w = self.last_w.get(r)
            if w is not None:
                deps.add(w)
        for r in writes:
            w = self.last_w.get(r)
            if w is not None:
                deps.add(w)
            for q in self.readers.get(r, ()):
                deps.add(q)
        if dkey is not None:
            p = self.last_dma.get(dkey)
            if p is not None:
                deps.add(p)
        idx = len(self.ops)
        fns = fn if isinstance(fn, (list, tuple)) else [fn]
        o = Op(idx, eng, fns, deps, dkey, cost, nbytes)
        o.prio = self.prio
        self.ops.append(o)
        for r in writes:
            self.last_w[r] = idx
            self.readers[r] = []
        for r in reads:
            if r not in writes:
                self.readers.setdefault(r, []).append(idx)
        if dkey is not None:
            self.last_dma[dkey] = idx
        return idx

    def alias(self, old_keys, new_keys):
        s = set()
        for k in old_keys:
            w = self.last_w.pop(k, None)
            if w is not None:
                s.add(w)
            s.update(self.readers.pop(k, ()))
        for k in new_keys:
            self.readers.setdefault(k, []).extend(s)

    def schedule(self):
        ops = self.ops
        n = len(ops)
        lists = {e: [o.idx for o in ops if o.eng == e] for e in ENGS}
        head = {e: 0 for e in ENGS}
        tfree = {e: 0.0 for e in ENGS}
        done = [False] * n
        finish = [0.0] * n
        order = {e: [] for e in ENGS}
        dma_t = 0.0
        remaining = n
        while remaining:
            best = None
            for e in ENGS:
                lst = lists[e]
                hd = head[e]
                while hd < len(lst) and done[lst[hd]]:
                    hd += 1
                head[e] = hd
                cnt = 0
                i = hd
                tf = tfree[e]
                W = SCHED_WINDOW[e]
                cand = None
                while i < len(lst) and cnt < W:
                    k = lst[i]
                    i += 1
                    if done[k]:
                        continue
                    cnt += 1
                    r = 0.0
                    ok = True
                    for d in ops[k].deps:
                        if not done[d]:
                            ok = False
                            break
                        f = finish[d]
                        if f > r:
                            r = f
                    if not ok:
                        continue
                    pr = ops[k].prio
                    if r <= tf:
                        key = (0, pr, k)
                    else:
                        key = (1, r, pr, k)
                    if cand is None or key < cand[0]:
                        cand = (key, r if r > tf else tf, k)
                    if r <= tf and pr == 0:
                        break
                if cand is not None:
                    st, k = cand[1], cand[2]
                    if best is None or st < best[0] or (st == best[0] and k < best[1]):
                        best = (st, k, e)
            st, k, e = best
            o = ops[k]
            if o.dkey is not None:
                tfree[e] = st + o.cost
                dma_t = max(dma_t, st + o.cost) + o.nbytes / 330e3
                finish[k] = dma_t + 2.0
            else:
                tfree[e] = st + o.cost
                finish[k] = st + o.cost + (0.25 if e == "pe" else 0.15)
            done[k] = True
            order[e].append(k)
            remaining -= 1
        self.order = order
        return max(finish) if n else 0.0

    def emit(self, nc, final_wait_ops=()):
        ops = self.ops
        order = self.order or {e: [o.idx for o in ops if o.eng == e] for e in ENGS}
        sem_deps = {}
        for o in ops:
            if o.eng == "pe" and o.dkey is None:
                sd = [d for d in o.deps if not (ops[d].eng == "pe" and ops[d].dkey is None)]
            else:
                sd = list(o.deps)
            sem_deps[o.idx] = sd
            for d in sd:
                ops[d].signal = True
        for d in final_wait_ops:
            ops[d].signal = True
        for o in ops:
            if o.dkey is not None:
                o.signal = True
        dkeys = sorted({o.dkey for o in ops if o.dkey is not None})
        with contextlib.ExitStack() as es:
            esem = {e: es.enter_context(nc.semaphore("s_" + e)) for e in ENGS}
            dsem = {k: es.enter_context(nc.semaphore("d_" + k)) for k in dkeys}
            dcount = {k: 0 for k in dkeys}
            for o in ops:
                if o.dkey is not None:
                    dcount[o.dkey] += 16
                    o.sem = dsem[o.dkey]
                    o.semval = dcount[o.dkey]
            for e in ENGS:
                c = 0
                for k in order[e]:
                    o = ops[k]
                    if o.signal and o.dkey is None:
                        c += 1
                        o.sem = esem[e]
                        o.semval = c
            block = es.enter_context(nc.Block())
            fin = [ops[d] for d in final_wait_ops]

            def make(ename, with_final):
                def body(eng):
                    waited = {}
                    for k in order[ename]:
                        o = ops[k]
                        need = {}
                        for d in sem_deps[k]:
                            p = ops[d]
                            kk = id(p.sem)
                            if p.semval > need.get(kk, (0, None))[0]:
                                need[kk] = (p.semval, p.sem)
                        for kk, (v, sm) in need.items():
                            if waited.get(kk, 0) >= v:
                                continue
                            eng.wait_ge(sm, v)
                            waited[kk] = v
                        ins = None
                        for fn in o.fns:
                            ins = fn(eng)
                        if o.signal:
                            ins.then_inc(o.sem, 16 if o.dkey is not None else 1)
                    if with_final:
                        for p in fin:
                            eng.wait_ge(p.sem, p.semval)
                return body

            block.tensor(make("pe", False))
            block.scalar(make("act", False))
            block.vector(make("dve", False))
            block.gpsimd(make("pool", False))
            block.sync(make("sp", True))


def build(layers=(0, 1, 2, 3), ngroups=4):
    nc = bass.Bass("TRN2", target_bir_lowering=False)
    nseq = (ngroups + 1) // 2
    hin = nc.dram_tensor("hin", [nseq, TSEQ, D], F32, kind="ExternalInput").ap()
    par_d = nc.dram_tensor("params", [128, NPAR], F32, kind="ExternalInput").ap()
    cnt_d = nc.dram_tensor("invcnt", [128, 4 * 16], F32, kind="ExternalInput").ap()
    idn_d = nc.dram_tensor("ident", [128, 128], F32, kind="ExternalInput").ap()
    pool_w = nc.dram_tensor("pool_w", [2, 4, 256, 256], F32, kind="ExternalInput").ap()
    sc_w_in = nc.dram_tensor("sc_w_in", [2, D, 3 * D], F32, kind="ExternalInput").ap()
    sc_w_out = nc.dram_tensor("sc_w_out", [2, D, D], F32, kind="ExternalInput").ap()
    w_up = nc.dram_tensor("ffn_w_up", [DEPTH, D, 2 * DFF], F32, kind="ExternalInput").ap()
    w_down = nc.dram_tensor("ffn_w_down", [DEPTH, DFF, D], F32, kind="ExternalInput").ap()
    out = nc.dram_tensor("out", [nseq, TSEQ, D], F32, kind="ExternalOutput").ap()

    P = Prog()
    es = contextlib.ExitStack()
    with es:
        def sb(name, shape, dt):
            return es.enter_context(nc.sbuf_tensor(name, shape, dt))

        h = sb("h", [128, KC, GRP], F32)
        actbuf = sb("actbuf", [128, JC * GRP // 2], F32)
        ugrp = sb("ugrp", [128, KC, GRP + HC], BF16)
        wa = sb("wa", [128, JC, D], BF16)
        ring = [sb(f"ring{i}", [128, 2 * KC * 384], BF16) for i in range(2)]
        sq = sb("sq", [128, KC, N + HC], BF16)
        NVEC = 16
        vbuf = sb("vbuf", [128, NVEC, N + HC], F32)
        vecs = [vbuf[:, i, :] for i in range(NVEC)]
        par = sb("par", [128, NPAR], F32)
        cnt = sb("cnt", [128, 4, 16], F32)
        idn = sb("idn", [128, 128], F32)
        ones = sb("ones", [128, 128], BF16)
        stg = [sb(f"stg{i}", [128, D], F32) for i in range(2)]
        halo_f = {l: sb(f"halof{l}", [128, KC, HC], BF16) for l in range(DEPTH)}
        halo_m = {}
        for l in range(DEPTH):
            if l % 2 == 0:
                halo_m[l] = sb(f"halom{l}", [128, KC, HP], F32)
            else:
                halo_m[l] = sb(f"halom{l}", [128, KC, HC], BF16)
        ps = [es.enter_context(nc.psum_tensor(f"ps{i}", [128, 512], F32)) for i in range(8)]

        ab16 = actbuf[:, :].bitcast(BF16)
        RW16 = JC * N
        RW32 = RW16 // 2
        act_t = [ab16[:, t * RW16:(t + 1) * RW16].rearrange("p (j n) -> p j n", j=JC) for t in range(NT)]
        y_t = [ab16[:, t * RW16:t * RW16 + KC * N].rearrange("p (j n) -> p j n", j=KC) for t in range(NT)]
        PW = 2 * (N + HP)
        assert 4 * PW <= RW32
        pscr = [[actbuf[:, t * RW32 + i * PW:t * RW32 + (i + 1) * PW].rearrange("p (c n) -> p c n", c=2)
                 for i in range(4)] for t in range(NT)]
        reg_keys = [{"act": [f"act{t}_{j}" for j in range(JC)],
                     "y": [f"y{t}_{m}" for m in range(KC)],
                     "pool": [f"pscr{t}_{i}" for i in range(4)]} for t in range(NT)]
        reg_mode = ["act"] * NT

        def region_to(t, mode):
            if reg_mode[t] != mode:
                P.alias(reg_keys[t][reg_mode[t]], reg_keys[t][mode])
                reg_mode[t] = mode

        cnts = {"psA": 0, "psB": 0, "psC": 0, "psC6": 0, "psS": 0, "A3": 0, "B3": 0, "S1c": 0, "S1v": 0, "S1b": 0, "vec": 0, "ring": 0, "stg": 0, "alt": 0}

        def g_ap(l, i, c):
            col = PC_G + (l * 4 + i) * 8 + c
            return par[:, col:col + 1]

        def ps_ap(j, c):
            col = PC_PS + j * 8 + c
            return par[:, col:col + 1]

        def sc_ap(j, k, c):
            col = PC_SC + (j * 3 + k) * 8 + c
            return par[:, col:col + 1]

        def fc_ap(l, k, jc):
            col = PC_FC + (l * 3 + k) * JC + jc
            return par[:, col:col + 1]

        P.op("sp", I("dma_start", out=par[:], in_=par_d), writes=["par"], dkey="par")
        P.op("sp", I("dma_start", out=cnt[:].rearrange("p g s -> p (g s)"), in_=cnt_d), writes=["cnt"], dkey="cnt")
        P.op("sp", I("dma_start", out=idn[:], in_=idn_d), writes=["idn"], dkey="idn")
        P.op("dve", I("memset", ones[:], 1.0), writes=["ones"])
        epsb = sb("epsb", [128, 1], F32)
        eps_ap = epsb[:, 0:1]
        P.op("dve", I("memset", epsb[:], EPS), writes=["eps"])

        def bank(role):
            if role in ("S", "T"):
                i = 6 + (cnts["psS"] % 2)
                cnts["psS"] += 1
                return i, f"ps{i}"
            if role == "C6":
                i = (4, 5, 0, 1, 2, 3)[cnts["psC6"] % 6]
                cnts["psC6"] += 1
                return i, f"ps{i}"
            if role in ("S1c", "S1v", "S1b"):
                bl = {"S1c": (0, 1), "S1v": (2,), "S1b": (3, 4, 5)}[role]
                i = bl[cnts[role] % len(bl)]
                cnts[role] += 1
                return i, f"ps{i}"
            if role in ("A3", "B3"):
                i = {"A3": (0, 1, 4), "B3": (2, 3, 5)}[role][cnts[role] % 3]
                cnts[role] += 1
                return i, f"ps{i}"
            base = {"A": 0, "B": 2, "C": 4}[role]
            k = "ps" + role
            i = base + (cnts[k] % 2)
            cnts[k] += 1
            return i, f"ps{i}"

        def vec():
            i = cnts["vec"] % NVEC
            cnts["vec"] += 1
            return vecs[i], f"vec{i}"

        def ring_slot():
            i = cnts["ring"] % 2
            cnts["ring"] += 1
            return ring[i], f"ring{i}"

        def load_wup_slot(l, sg):
            j0 = 3 * sg
            nj = min(3, JC - j0)
            slot, key = ring_slot()
            sv = slot[:, 0:2 * KC * 384].rearrange("p (g k c) -> p g k c", g=2, k=KC)
            for gv in range(2):
                c0 = gv * DFF + j0 * 128
                src = w_up[l, :, c0:c0 + nj * 128].rearrange("(k p) c -> p k c", p=128)
                dst = sv[:, gv, :, 0:nj * 128]
                P.op("pool", I("dma_start", out=dst, in_=src),
                     writes=[f"{key}_b{u}" for u in range(3 * gv, 3 * gv + 3)], dkey=key + f"_{gv}", cost=1.0,
                     nbytes=nj * 128 * D * 4)
            return sv, key, j0, nj

        def load_wa_down(l):
            bounds = [0, 6, 12, 17, 22]
            for pi in range(4):
                a, b = bounds[pi], bounds[pi + 1]
                src = w_down[l, a * 128:b * 128, :].rearrange("(j p) c -> p j c", p=128)
                dst = wa[:, a:b, :]
                P.op("pool", I("dma_start", out=dst, in_=src),
                     writes=[f"wa{pi}"], dkey=f"wa{pi}", cost=1.0, nbytes=(b - a) * 128 * D * 4)

        def load_wa_out(jl):
            for pi, (a, b) in enumerate(((0, 6), (6, 8))):
                src = sc_w_out[jl, a * 128:b * 128, :].rearrange("(j p) c -> p j c", p=128)
                dst = wa[:, a:b, :]
                P.op("pool", I("dma_start", out=dst, in_=src),
                     writes=[f"wa{pi}"], dkey=f"wa{pi}", cost=1.0, nbytes=(b - a) * 128 * D * 4)

        def load_win_slot(jl, sg):
            slot, key = ring_slot()
            sv = slot[:, 0:3 * KC * 256].rearrange("p (g k c) -> p g k c", g=3, k=KC)
            for gi in range(3):
                c0 = gi * D + sg * 256
                src = sc_w_in[jl, :, c0:c0 + 256].rearrange("(k p) c -> p k c", p=128)
                dst = sv[:, gi, :, :]
                P.op("pool", I("dma_start", out=dst, in_=src),
                     writes=[f"{key}_b{u}" for u in range(2 * gi, 2 * gi + 2)], dkey=key + f"_{gi}", cost=1.0,
                     nbytes=256 * D * 4)
            return sv, key

        pwres = sb("pwres", [128, 2, 4, 2, 256], BF16)
        for jl in range(2):
            for g in range(4):
                P.op("pool", I("dma_start", out=pwres[:, jl, g, :, :], in_=pool_w[jl, g].rearrange("(k p) c -> p k c", p=128)),
                     writes=[f"pw{jl}"], dkey=f"pw{jl}", cost=1.0, nbytes=256 * 256 * 4)

        def rstd_from_ss(sbank, skey, ncols):
            t1, k1 = vec()
            P.op("act", I("activation", out=t1[:, 0:ncols], in_=ps[sbank][:, 0:ncols], func=AF.Ln,
                          bias=eps_ap, scale=1.0 / D),
                 reads=[skey, "eps"], writes=[k1], cost=1.9)
            P.op("act", I("activation", out=t1[:, 0:ncols], in_=t1[:, 0:ncols], func=AF.Exp, scale=-0.5),
                 reads=[k1], writes=[k1], cost=1.9)
            return t1, k1

        def hk(t):
            return f"h{t}"

        sq_mode = ["whole"]

        def sq_to(mode):
            if sq_mode[0] == mode:
                return
            if mode == "chunks":
                P.alias(["sq0", "sq1"], [f"sqm{m}" for m in range(KC)])
            else:
                P.alias([f"sqm{m}" for m in range(KC)], ["sq0", "sq1"])
            sq_mode[0] = mode

        def norm_stats(t):
            o = t * N
            sq_to("whole")
            for hf in range(2):
                P.op("act", I("activation", out=sq[:, 4 * hf:4 * hf + 4, 0:N], in_=h[:, 4 * hf:4 * hf + 4, o:o + N],
                              func=AF.Square),
                     reads=[hk(t)], writes=[f"sq{hf}"], cost=1.25)
            sbank, skey = bank("S")
            for hf in range(2):
                P.op("pe", [I("matmul", ps[sbank][:, 0:N], lhsT=ones[:], rhs=sq[:, c, 0:N],
                              start=(c == 0), stop=(c == KC - 1)) for c in range(4 * hf, 4 * hf + 4)],
                     reads=[f"sq{hf}", "ones"], writes=[skey], cost=4 * MMC)
            return rstd_from_ss(sbank, skey, N)

        def prenorm_ugrp(l, i, t):
            o = t * N
            r, kr = norm_stats(t)
            for c in range(KC):
                P.op("dve", I("scalar_tensor_tensor", out=ugrp[:, c, HC + o:HC + o + N], in0=h[:, c, o:o + N],
                              scalar=g_ap(l, i, c), in1=r[:, 0:N], op0=ALU.mult, op1=ALU.mult),
                     reads=[hk(t), kr, "par"], writes=[f"ug{t}"], cost=0.5)

        class PostNorm:
            def __init__(self, l, i, t, scale_fn=None):
                self.l, self.i, self.t = l, i, t
                self.scale_fn = scale_fn
                sq_to("chunks")
                self.sbank, self.skey = bank("S")
                self.bufs = {}

            def chunk(self, m, pbank, pkey):
                sc = self.scale_fn(m) if self.scale_fn is not None else 1.0
                rd = [pkey] + (["par"] if self.scale_fn is not None else [])
                fr, kf = vec()
                self.bufs[m] = (fr, kf)
                P.op("act", I("activation", out=sq[:, m, 0:N], in_=ps[pbank][:, 0:N], func=AF.Square, scale=sc),
                     reads=rd, writes=[f"sqm{m}"], cost=0.45)
                P.op("act", I("activation", out=fr[:, 0:N], in_=ps[pbank][:, 0:N], func=AF.Copy, scale=sc),
                     reads=rd, writes=[kf], cost=0.55)
                P.op("pe", I("matmul", ps[self.sbank][:, 0:N], lhsT=ones[:], rhs=sq[:, m, 0:N],
                             start=(m == 0), stop=(m == KC - 1)),
                     reads=[f"sqm{m}", "ones"], writes=[self.skey], cost=MMC)

            def finish(self):
                l, i, t = self.l, self.i, self.t
                o = t * N
                r, kr = rstd_from_ss(self.sbank, self.skey, N)
                for m in range(KC):
                    fr, kf = self.bufs[m]
                    P.op("dve", I("scalar_tensor_tensor", out=fr[:, 0:N], in0=fr[:, 0:N], scalar=g_ap(l, i, m),
                                  in1=r[:, 0:N], op0=ALU.mult, op1=ALU.mult),
                         reads=[kf, kr, "par"], writes=[kf], cost=0.58)
                    P.op("pool", I("tensor_tensor", out=h[:, m, o:o + N], in0=h[:, m, o:o + N], in1=fr[:, 0:N], op=ALU.add),
                         reads=[kf, hk(t)], writes=[hk(t)], cost=0.95)

        NBLK = (GRP + 127) // 128

        def blk(tb):
            r0 = tb * 128
            return r0, min(128, GRP - r0)

        def tiles_of(tb):
            r0, nr = blk(tb)
            return sorted({min(NT - 1, r // N) for r in (r0, r0 + nr - 1)})

        def load_block(gi, tb):
            b, half = gi // 2, gi % 2
            t0 = half * GRP
            r0, nr = blk(tb)
            si = cnts["stg"] % 2
            cnts["stg"] += 1
            s = stg[si]
            P.op("sp", I("dma_start", out=s[0:nr, :], in_=hin[b, t0 + r0:t0 + r0 + nr, :]),
                 writes=[f"stg{si}"], dkey=f"ld{si}", cost=0.1, nbytes=nr * D * 4)
            hkeys = [hk(t) for t in tiles_of(tb)]
            for hf in range(2):
                tb_i, tkey = bank("T")
                pv = ps[tb_i][:, 0:4 * 128].rearrange("p (c t) -> p c t", c=4)
                P.op("pe", [I("transpose", out=pv[:, cc, 0:nr], in_=s[0:nr, (hf * 4 + cc) * 128:(hf * 4 + cc + 1) * 128],
                              identity=idn[0:nr, 0:nr]) for cc in range(4)],
                     reads=[f"stg{si}", "idn"], writes=[tkey], cost=1.2)
                P.op("act", I("activation", out=h[:, hf * 4:hf * 4 + 4, r0:r0 + nr], in_=pv[:, :, 0:nr], func=AF.Copy),
                     reads=[tkey], writes=hkeys, cost=0.6)

        out_dmas = {}

        def store_block(gi, tb):
            b, half = gi // 2, gi % 2
            t0 = half * GRP
            r0, nr = blk(tb)
            si = cnts["stg"] % 2
            cnts["stg"] += 1
            s = stg[si]
            hkeys = [hk(t) for t in tiles_of(tb)]
            for hf in range(2):
                tb_i, tkey = bank("T")
                P.op("pe", [I("transpose", out=ps[tb_i][0:nr, cc * 128:(cc + 1) * 128], in_=h[:, hf * 4 + cc, r0:r0 + nr],
                              identity=idn[:]) for cc in range(4)],
                     reads=hkeys + ["idn"], writes=[tkey], cost=1.2)
                P.op("act", I("activation", out=s[0:nr, hf * 512:(hf + 1) * 512], in_=ps[tb_i][0:nr, 0:512], func=AF.Copy),
                     reads=[tkey], writes=[f"stg{si}"], cost=0.6)
            d = P.op("sp", I("dma_start", out=out[b, t0 + r0:t0 + r0 + nr, :], in_=s[0:nr, :]),
                     reads=[f"stg{si}"], dkey=f"st{si}", cost=0.1, nbytes=nr * D * 4)
            out_dmas[f"st{si}"] = d

        def ugrp_halo_fill(src, srckey, half):
            if half == 0:
                P.op("dve", I("memset", ugrp[:, :, 0:HC], 0.0), writes=["ugh"], cost=0.1)
            else:
                P.op("dve", I("tensor_copy", out=ugrp[:, :, 0:HC], in_=src[:, :, :]),
                     reads=[srckey], writes=["ugh"], cost=0.1)

        def ugrp_halo_save(dst, dstkey):
            P.op("dve", I("tensor_copy", out=dst[:, :, :], in_=ugrp[:, :, GRP:GRP + HC]),
                 reads=[f"ug{NT - 1}"], writes=[dstkey], cost=0.1)

        def ffn_pre_tile(l, t, half):
            if t == 0:
                ugrp_halo_fill(halo_f[l], f"halof{l}", half)
            prenorm_ugrp(l, 2, t)
            if t == NT - 1:
                ugrp_halo_save(halo_f[l], f"halof{l}")

        def sconv_pre_tile(l, t, half):
            if t == 0:
                ugrp_halo_fill(halo_m[l], f"halom{l}", half)
            prenorm_ugrp(l, 0, t)
            if t == NT - 1:
                ugrp_halo_save(halo_m[l], f"halom{l}")

        def ukeys_of(t):
            return [f"ug{t}"] + ([f"ug{t - 1}"] if t > 0 else ["ugh"])

        NSG = (JC + 2) // 3
        pre_wup = {}
        pre_win = {}

        def ffn_prefetch(l):
            pre_wup[l] = [load_wup_slot(l, 0), load_wup_slot(l, 1)]

        def sconv_prefetch(l):
            pre_win[l] = [load_win_slot(l // 2, 0), load_win_slot(l // 2, 1)]

        def ffn_u(l):
            for t in range(NT):
                region_to(t, "act")
            load_wa_down(l)
            if l not in pre_wup:
                ffn_prefetch(l)
            slots = pre_wup.pop(l)
            nsg = NSG
            for sg in range(nsg):
                sv, key, j0, nj = slots[sg]
                for t in range(NT):
                    o = t * N
                    for jj in range(nj):
                        j = j0 + jj
                        ga, gk = bank("A3")
                        va, vk = bank("B3")
                        P.op("pe", [I("matmul", ps[ga][:, 0:N + HC], lhsT=sv[:, 0, k, jj * 128:(jj + 1) * 128],
                                      rhs=ugrp[:, k, o:o + N + HC], start=(k == 0), stop=(k == KC - 1)) for k in range(KC)],
                             reads=ukeys_of(t) + [f"{key}_b{u}" for u in range(0, 3)], writes=[gk], cost=KC * MMC)
                        P.op("pe", [I("matmul", ps[va][:, 0:N], lhsT=sv[:, 1, k, jj * 128:(jj + 1) * 128],
                                      rhs=ugrp[:, k, o + HC:o + HC + N], start=(k == 0), stop=(k == KC - 1)) for k in range(KC)],
                             reads=[f"ug{t}"] + [f"{key}_b{u}" for u in range(3, 6)], writes=[vk], cost=KC * MMC)
                        a, ka = vec()
                        P.op("act", I("activation", out=a[:, 0:N], in_=ps[ga][:, 2:N + 2], func=AF.Copy, scale=fc_ap(l, 2, j)),
                             reads=[gk, "par"], writes=[ka], cost=0.5)
                        P.op("dve", I("scalar_tensor_tensor", out=a[:, 0:N], in0=ps[ga][:, 1:N + 1], scalar=fc_ap(l, 1, j),
                                      in1=a[:, 0:N], op0=ALU.mult, op1=ALU.add),
                             reads=[gk, "par", ka], writes=[ka], cost=0.5)
                        P.op("dve", I("scalar_tensor_tensor", out=a[:, 0:N], in0=ps[ga][:, 0:N], scalar=fc_ap(l, 0, j),
                                      in1=a[:, 0:N], op0=ALU.mult, op1=ALU.add),
                             reads=[gk, "par", ka], writes=[ka], cost=0.5)
                        s_, ks = vec()
                        P.op("act", I("activation", out=s_[:, 0:N], in_=a[:, 0:N], func=AF.Silu),
                             reads=[ka], writes=[ks], cost=0.5)
                        P.op("dve", I("tensor_tensor", out=act_t[t][:, j, :], in0=s_[:, 0:N], in1=ps[va][:, 0:N], op=ALU.mult),
                             reads=[ks, vk], writes=[f"act{t}_{j}"], cost=0.5)
                if sg + 2 < nsg:
                    slots.append(load_wup_slot(l, sg + 2))

        def ffn_d_tile(l, t):
            pn = PostNorm(l, 3, t)
            for m in range(KC):
                fb, fk = bank("C6")
                P.op("pe", [I("matmul", ps[fb][:, 0:N], lhsT=wa[:, j, m * 128:(m + 1) * 128], rhs=act_t[t][:, j, :],
                              start=(j == 0), stop=(j == JC - 1)) for j in range(JC)],
                     reads=[f"act{t}_{j}" for j in range(JC)] + ["wa0", "wa1", "wa2", "wa3"], writes=[fk], cost=JC * MMC)
                pn.chunk(m, fb, fk)
            pn.finish()

        def sconv_s1(l):
            jl = l // 2
            for t in range(NT):
                region_to(t, "y")
            if l not in pre_win:
                sconv_prefetch(l)
            slots = pre_win.pop(l)
            load_wa_out(jl)
            for sg in range(4):
                sv, key = slots[sg]
                for t in range(NT):
                    o = t * N
                    for mm in range(2):
                        m = 2 * sg + mm
                        ca, ck = bank("S1c")
                        va, vk = bank("S1v")
                        ba, bk = bank("S1b")
                        for gi, (bb, ncol, off) in enumerate(((ba, N, HC), (ca, N + HC, 0), (va, N + HC, 0))):
                            P.op("pe", [I("matmul", ps[bb][:, 0:ncol], lhsT=sv[:, gi, k, mm * 128:(mm + 1) * 128],
                                          rhs=ugrp[:, k, o + off:o + off + ncol], start=(k == 0), stop=(k == KC - 1))
                                        for k in range(KC)],
                                 reads=ukeys_of(t) + [f"{key}_b{u}" for u in range(2 * gi, 2 * gi + 2)],
                                 writes=[(bk, ck, vk)[gi]], cost=KC * MMC)
                        vs, kvs = vec()
                        P.op("act", I("activation", out=vs[:, 0:N + HC], in_=ps[va][:, 0:N + HC], func=AF.Copy),
                             reads=[vk], writes=[kvs], cost=0.5)
                        pr, kp = vec()
                        P.op("dve", I("tensor_tensor", out=pr[:, 0:N + HC], in0=vs[:, 0:N + HC], in1=ps[ca][:, 0:N + HC], op=ALU.mult),
                             reads=[kvs, ck], writes=[kp], cost=0.5)
                        a, ka = vec()
                        P.op("act", I("activation", out=a[:, 0:N], in_=pr[:, 2:N + 2], func=AF.Copy, scale=sc_ap(jl, 2, m)),
                             reads=[kp, "par"], writes=[ka], cost=0.5)
                        P.op("dve", I("scalar_tensor_tensor", out=a[:, 0:N], in0=pr[:, 1:N + 1], scalar=sc_ap(jl, 1, m),
                                      in1=a[:, 0:N], op0=ALU.mult, op1=ALU.add),
                             reads=[kp, "par", ka], writes=[ka], cost=0.5)
                        P.op("dve", I("scalar_tensor_tensor", out=a[:, 0:N], in0=pr[:, 0:N], scalar=sc_ap(jl, 0, m),
                                      in1=a[:, 0:N], op0=ALU.mult, op1=ALU.add),
                             reads=[kp, "par", ka], writes=[ka], cost=0.5)
                        P.op("dve", I("tensor_tensor", out=y_t[t][:, m, :], in0=a[:, 0:N], in1=ps[ba][:, 0:N], op=ALU.mult),
                             reads=[ka, bk], writes=[f"y{t}_{m}"], cost=0.5)
                if sg + 2 < 4:
                    slots.append(load_win_slot(jl, sg + 2))

        def sconv_s2_tile(l, t):
            pn = PostNorm(l, 1, t)
            for m in range(KC):
                fb, fk = bank("C6")
                P.op("pe", [I("matmul", ps[fb][:, 0:N], lhsT=wa[:, k, m * 128:(m + 1) * 128], rhs=y_t[t][:, k, :],
                              start=(k == 0), stop=(k == KC - 1)) for k in range(KC)],
                     reads=[f"y{t}_{k}" for k in range(KC)] + ["wa0", "wa1"], writes=[fk], cost=KC * MMC)
                pn.chunk(m, fb, fk)
            pn.finish()

        pool_r = {}

        def pool_stage_a(l, t, half):
            region_to(t, "pool")
            pool_r[t] = norm_stats(t)

        def pool_stage_b(l, t, half):
            o = t * N
            first = (half == 0 and t == 0)
            r, kr = pool_r[t]
            for g in range(4):
                W = POOL_W[g]
                pu = pscr[t][g % 2]
                pukey = f"pscr{t}_{g % 2}"
                hkey = f"halom{l}_{g}"
                if first:
                    P.op("dve", I("memset", pu[:, :, 0:HP], 0.0), writes=[pukey], cost=0.1)
                else:
                    P.op("dve", I("tensor_copy", out=pu[:, :, 0:HP], in_=halo_m[l][:, 2 * g:2 * g + 2, :]),
                         reads=[hkey], writes=[pukey], cost=0.1)
                for cc in range(2):
                    c = 2 * g + cc
                    P.op("dve", I("scalar_tensor_tensor", out=pu[:, cc, HP:HP + N], in0=h[:, c, o:o + N],
                                  scalar=g_ap(l, 0, c), in1=r[:, 0:N], op0=ALU.mult, op1=ALU.mult),
                         reads=[hk(t), kr, "par", pukey], writes=[pukey], cost=0.58)
                P.op("dve", I("tensor_copy", out=halo_m[l][:, 2 * g:2 * g + 2, :], in_=pu[:, :, N:N + HP]),
                     reads=[pukey], writes=[hkey], cost=0.1)
                cur, lo, curkey = pu, 0, pukey
                for s_ in range(g + 1):
                    sh = 1 << s_
                    nlo = HP - (W - 2 * sh)
                    ln = HP + N - nlo
                    dstb = pscr[t][2 + (s_ % 2)]
                    dkey_ = f"pscr{t}_{2 + (s_ % 2)}"
                    eng = "dve"
                    P.op(eng, I("tensor_tensor", out=dstb[:, :, 0:ln], in0=cur[:, :, nlo - lo:nlo - lo + ln],
                                in1=cur[:, :, nlo - lo - sh:nlo - lo - sh + ln], op=ALU.add),
                         reads=[curkey], writes=[dkey_], cost=0.9)
                    cur, lo, curkey = dstb, nlo, dkey_
                sfin = cur[:, :, HP - lo:HP - lo + N]
                pl = ugrp[:, 2 * g:2 * g + 2, HC + o:HC + o + N]
                P.op("dve", I("scalar_tensor_tensor", out=pl, in0=sfin, scalar=1.0 / W,
                              in1=pu[:, :, HP:HP + N], op0=ALU.mult, op1=ALU.subtract),
                     reads=[curkey, pukey], writes=[f"ug{t}"], cost=0.9)
                if first:
                    for cc in range(2):
                        tmpv, kt = vec()
                        P.op("dve", I("tensor_tensor", out=tmpv[:, 0:16], in0=sfin[:, cc, 0:16], in1=cnt[:, g, :], op=ALU.mult),
                             reads=[curkey, "cnt"], writes=[kt], cost=0.1)
                        P.op("dve", I("tensor_tensor", out=ugrp[:, 2 * g + cc, HC + o:HC + o + 16], in0=tmpv[:, 0:16],
                                      in1=pu[:, cc, HP:HP + 16], op=ALU.subtract),
                             reads=[kt, pukey], writes=[f"ug{t}"], cost=0.1)

        def pool_stage_c(l, t, half):
            jl = l // 2
            o = t * N
            pn = PostNorm(l, 1, t, scale_fn=lambda m: ps_ap(jl, m))
            for g in range(4):
                for dh in range(2):
                    fb, fk = bank("C6")
                    P.op("pe", [I("matmul", ps[fb][:, 0:N], lhsT=pwres[:, jl, g, kc, dh * 128:(dh + 1) * 128],
                                  rhs=ugrp[:, 2 * g + kc, HC + o:HC + o + N], start=(kc == 0), stop=(kc == 1)) for kc in range(2)],
                         reads=[f"ug{t}", f"pw{jl}"], writes=[fk], cost=2 * MMC)
                    pn.chunk(2 * g + dh, fb, fk)
            pn.finish()

        def pool_phase(l, half):
            stages = (pool_stage_a, pool_stage_b, pool_stage_c, ffn_pre_tile)
            for step in range(NT + len(stages) - 1):
                for k, st in enumerate(stages):
                    t = step - k
                    if 0 <= t < NT:
                        st(l, t, half)

        def pool_tile(l, t, half):
            pool_stage_a(l, t, half)
            pool_stage_b(l, t, half)
            pool_stage_c(l, t, half)

        def pre_work(gi, li, t):
            l, half = layers[li], gi % 2
            P.prio = 1
            if l % 2 == 0:
                pool_tile(l, t, half)
                ffn_pre_tile(l, t, half)
            else:
                sconv_pre_tile(l, t, half)
            P.prio = 0

        if INTERLEAVE:
            for tb in range(NBLK):
                load_block(0, tb)
            for t in range(NT):
                pre_work(0, 0, t)
            for gi in range(ngroups):
                half = gi % 2
                stored = 0
                pre_done = 0
                for li, l in enumerate(layers):
                    if l % 2 == 1:
                        sconv_s1(l)
                        for t in range(NT):
                            sconv_s2_tile(l, t)
                            ffn_pre_tile(l, t, half)
                    ffn_u(l)
                    for t in range(NT):
                        ffn_d_tile(l, t)
                        if li + 1 < len(layers):
                            pre_work(gi, li + 1, t)
                        else:
                            P.prio = 1
                            while stored < NBLK and max(tiles_of(stored)) <= t:
                                store_block(gi, stored)
                                if gi + 1 < ngroups:
                                    load_block(gi + 1, stored)
                                stored += 1
                            P.prio = 0
                            if gi + 1 < ngroups:
                                while pre_done < NT and (pre_done + 1) * N <= min(GRP, stored * 128):
                                    pre_work(gi + 1, 0, pre_done)
                                    pre_done += 1
        else:
            for gi in range(ngroups):
                half = gi % 2
                for tb in range(NBLK):
                    load_block(gi, tb)
                for li, l in enumerate(layers):
                    if l % 2 == 0:
                        ffn_prefetch(l)
                        pool_phase(l, half)
                    else:
                        for t in range(NT):
                            sconv_pre_tile(l, t, half)
                        sconv_s1(l)
                        ffn_prefetch(l)
                        for t in range(NT):
                            sconv_s2_tile(l, t)
                        for t in range(NT):
                            ffn_pre_tile(l, t, half)
                    ffn_u(l)
                    if li + 1 < len(layers) and layers[li + 1] % 2 == 1:
                        sconv_prefetch(layers[li + 1])
                    for t in range(NT):
                        ffn_d_tile(l, t)
                for tb in range(NBLK):
                    store_block(gi, tb)

        if SCHEDULE:
            P.schedule()
        P.emit(nc, final_wait_ops=list(out_dmas.values()))
    return nc


def _host_tables(pool_scale, sc_conv, ffn_conv, norm_g):
    par = np.zeros((128, NPAR), np.float32)
    par[:, PC_G:PC_G + 128] = np.asarray(norm_g, np.float32).reshape(DEPTH * 4, KC, 128).transpose(2, 0, 1).reshape(128, 128)
    par[:, PC_PS:PC_PS + 16] = np.asarray(pool_scale, np.float32).reshape(2, KC, 128).transpose(2, 0, 1).reshape(128, 16)
    par[:, PC_SC:PC_SC + 48] = np.asarray(sc_conv, np.float32).reshape(2 * 3, KC, 128).transpose(2, 0, 1).reshape(128, 48)
    par[:, PC_FC:PC_FC + DEPTH * 3 * JC] = np.asarray(ffn_conv, np.float32).reshape(DEPTH * 3, JC, 128).transpose(2, 0, 1).reshape(128, DEPTH * 3 * JC)
    cnt = np.zeros((128, 4, 16), np.float32)
    for g, w in enumerate(POOL_W):
        cnt[:, g, :] = 1.0 / np.minimum(np.arange(16) + 1.0, float(w))
    ident = np.eye(128, dtype=np.float32)
    return par, cnt.reshape(128, 64), ident


_NC_CACHE = {}


def _get_nc(layers, ngroups):
    key = (tuple(layers), ngroups)
    if key not in _NC_CACHE:
        _NC_CACHE[key] = build(layers, ngroups)
    return _NC_CACHE[key]


LAUNCH_PLAN = [(0, 1, 2, 3)]


def kernel(x, meta_tokens, pool_w, pool_scale, sc_w_in, sc_conv, sc_w_out,
           ffn_w_up, ffn_conv, ffn_w_down, norm_g):
    x = np.asarray(x, np.float32)
    B = x.shape[0]
    ncores = 8
    meta = np.broadcast_to(np.asarray(meta_tokens, np.float32)[None], (B, NMETA, D))
    hcur = np.ascontiguousarray(np.concatenate([meta, x], axis=1))
    par, cnt, ident = _host_tables(pool_scale, sc_conv, ffn_conv, norm_g)
    common = {
        "params": par, "invcnt": cnt, "ident": ident,
        "pool_w": np.ascontiguousarray(pool_w, np.float32),
        "sc_w_in": np.ascontiguousarray(sc_w_in, np.float32),
        "sc_w_out": np.ascontiguousarray(sc_w_out, np.float32),
        "ffn_w_up": np.ascontiguousarray(ffn_w_up, np.float32),
        "ffn_w_down": np.ascontiguousarray(ffn_w_down, np.float32),
    }
    per = B // ncores
    for layers in LAUNCH_PLAN:
        nc = _get_nc(layers, 2 * per)
        in_maps = []
        for c in range(ncores):
            m = dict(common)
            m["hin"] = np.ascontiguousarray(hcur[c * per:(c + 1) * per])
            in_maps.append(m)
        res = run_bass_kernel_spmd(nc, in_maps, core_ids=list(range(ncores)))
        hcur = np.concatenate([np.asarray(r["out"], np.float32) for r in res.results], axis=0)
    return np.ascontiguousarray(hcur[:, NMETA:, :])
```
